# Optimizing a Trainium2 kernel written in Bass

```python
import math
import jax, jax.numpy as jnp
from jax import lax
import numpy as np

D_MODEL = 1024
BATCH = 4
SEQ = 4096
DEPTH = 4

HEAD_DIM = 64
SWA_HEADS = 8
SWA_KV_HEADS = 2
SWA_GROUP = SWA_HEADS // SWA_KV_HEADS
WINDOW = 128
DIFF_HEADS = 4
DIFF_VDIM = 2 * HEAD_DIM
BLOCK = 128
D_FF = 2816
CONV_WIDTH = 3
RMS_EPS = 1e-6
NEG_INF = -1e30

SWA_Q = SWA_HEADS * HEAD_DIM
SWA_KV = SWA_KV_HEADS * HEAD_DIM
DIFF_QK = DIFF_HEADS * 2 * HEAD_DIM
DIFF_V = DIFF_HEADS * DIFF_VDIM
IN_COLS = SWA_Q + 2 * SWA_KV + 2 * DIFF_QK + DIFF_V + 2 * D_MODEL

kernel_name = "hybrid_swa_sink_diffattn_alibi_convffn_adaln"


def _rmsnorm(x, g):
    xf = x.astype(jnp.float32)
    y = xf * lax.rsqrt(jnp.mean(xf * xf, axis=-1, keepdims=True) + RMS_EPS)
    return (y * g.astype(jnp.float32)).astype(x.dtype)


def _alibi_slopes(n):
    return jnp.array([2.0 ** (-8.0 * (i + 1) / n) for i in range(n)], dtype=jnp.float32)


def _sliding_window_attention(q, k, v, sinks, slopes):
    b, s, _, d = q.shape
    nb = s // BLOCK
    qb = q.reshape(b, nb, BLOCK, SWA_KV_HEADS, SWA_GROUP, d)

    def with_prev(t):
        tb = t.reshape(b, nb, BLOCK, SWA_KV_HEADS, d)
        prev = jnp.concatenate([jnp.zeros_like(tb[:, :1]), tb[:, :-1]], axis=1)
        return jnp.concatenate([prev, tb], axis=2)

    kb, vb = with_prev(k), with_prev(v)
    scores = jnp.einsum('bnqkgd,bnskd->bnkgqs', qb, kb).astype(jnp.float32) * (d ** -0.5)
    qi = jnp.arange(BLOCK)[:, None]
    kj = jnp.arange(2 * BLOCK)[None, :]
    delta = qi + BLOCK - kj
    key_pos = jnp.arange(nb)[:, None, None] * BLOCK - BLOCK + kj[None]
    valid = (delta >= 0) & (delta < WINDOW) & (key_pos >= 0)
    bias = -slopes[:, :, None, None] * delta.astype(jnp.float32)
    scores = jnp.where(valid[None, :, None, None], scores + bias, NEG_INF)
    sink = jnp.broadcast_to(sinks.astype(jnp.float32).reshape(SWA_KV_HEADS, SWA_GROUP, 1, 1),
                            scores.shape[:-1] + (1,))
    probs = jax.nn.softmax(jnp.concatenate([scores, sink], axis=-1), axis=-1)[..., :-1]
    out = jnp.einsum('bnkgqs,bnskd->bnqkgd', probs.astype(v.dtype), vb)
    return out.reshape(b, s, SWA_HEADS * d)


def _diff_attention(q, k, v, lam, lam_init, subln_g, slopes):
    b, s = q.shape[:2]
    nb = s // BLOCK
    qb = jnp.moveaxis(q.reshape(b, nb, BLOCK, DIFF_HEADS, 2, HEAD_DIM), 1, 0)
    key_pos = jnp.arange(s)
    scale = HEAD_DIM ** -0.5

    def block_fn(args):
        qblk, n = args
        scores = jnp.einsum('bqhmd,bshmd->bhmqs', qblk, k).astype(jnp.float32) * scale
        q_pos = n * BLOCK + jnp.arange(BLOCK)
        delta = q_pos[:, None] - key_pos[None, :]
        scores = scores - slopes[:, None, None, None] * delta.astype(jnp.float32)
        scores = jnp.where(delta >= 0, scores, NEG_INF)
        p = jax.nn.softmax(scores, axis=-1)
        a = p[:, :, 0] - lam * p[:, :, 1]
        return jnp.einsum('bhqs,bshe->bqhe', a.astype(v.dtype), v)

    out = lax.map(block_fn, (qb, jnp.arange(nb)))
    out = jnp.moveaxis(out, 0, 1).reshape(b, s, DIFF_HEADS, DIFF_VDIM)
    out = _rmsnorm(out, subln_g) * (1.0 - lam_init)
    return out.reshape(b, s, DIFF_V)


def _conv_ffn(h, w_up, conv_w, conv_b, w_down):
    u = h @ w_up
    a, g = jnp.split(u, 2, axis=-1)
    s = a.shape[1]
    ap = jnp.pad(a, ((0, 0), (CONV_WIDTH - 1, 0), (0, 0)))
    a = ap[:, 0:s] * conv_w[0] + ap[:, 1:s + 1] * conv_w[1] + ap[:, 2:s + 2] * conv_w[2] + conv_b
    return (jax.nn.gelu(a, approximate=False) * g) @ w_down


def setup_inputs(seed: int = 0) -> dict:
    key = jax.random.key(seed)
    ks = jax.random.split(key, 20)
    nrm = lambda k, shape, s: jax.random.normal(k, shape, jnp.float32) * s
    D = D_MODEL
    return {
        "x": nrm(ks[0], (BATCH, SEQ, D), 1.0),
        "c": nrm(ks[1], (BATCH, D), 1.0),
        "w_ada": nrm(ks[2], (DEPTH, D, 6 * D), 0.5 * D ** -0.5),
        "b_ada": nrm(ks[3], (DEPTH, 6 * D), 0.02),
        "norm1_g": 1.0 + nrm(ks[4], (DEPTH, D), 0.02),
        "w_in": nrm(ks[5], (DEPTH, D, IN_COLS), D ** -0.5),
        "swa_sinks": nrm(ks[6], (DEPTH, SWA_HEADS), 1.0),
        "diff_lambda": nrm(ks[7], (DEPTH, 4, HEAD_DIM), 0.1),
        "diff_subln_g": 1.0 + nrm(ks[8], (DEPTH, DIFF_VDIM), 0.02),
        "w_branch_swa": nrm(ks[9], (DEPTH, SWA_Q, D), SWA_Q ** -0.5),
        "w_branch_diff": nrm(ks[10], (DEPTH, DIFF_V, D), DIFF_V ** -0.5),
        "w_out": nrm(ks[11], (DEPTH, D, D), D ** -0.5),
        "norm2_g": 1.0 + nrm(ks[12], (DEPTH, D), 0.02),
        "w_up": nrm(ks[13], (DEPTH, D, 2 * D_FF), D ** -0.5),
        "conv_w": nrm(ks[14], (DEPTH, CONV_WIDTH, D_FF), CONV_WIDTH ** -0.5),
        "conv_b": nrm(ks[15], (DEPTH, D_FF), 0.02),
        "w_down": nrm(ks[16], (DEPTH, D_FF, D), D_FF ** -0.5),
        "final_g": 1.0 + nrm(ks[17], (D,), 0.02),
    }


def reference(x, c, w_ada, b_ada, norm1_g, w_in, swa_sinks, diff_lambda, diff_subln_g,
              w_branch_swa, w_branch_diff, w_out, norm2_g, w_up, conv_w, conv_b, w_down, final_g):
    b, s, _ = x.shape
    sizes = [SWA_Q, SWA_KV, SWA_KV, DIFF_QK, DIFF_QK, DIFF_V, D_MODEL, D_MODEL]
    points = [int(p) for p in np.cumsum(sizes)[:-1]]
    slopes_swa = _alibi_slopes(SWA_HEADS).reshape(SWA_KV_HEADS, SWA_GROUP)
    slopes_diff = _alibi_slopes(DIFF_HEADS)
    cs = jax.nn.silu(c)
    for l in range(DEPTH):
        mod = cs @ w_ada[l] + b_ada[l]
        shift1, scale1, gate1, shift2, scale2, gate2 = [m[:, None, :] for m in jnp.split(mod, 6, axis=-1)]

        h = _rmsnorm(x, norm1_g[l]) * (1.0 + scale1) + shift1
        proj = h @ w_in[l]
        qa, ka, va, qd, kd, vd, ga, gd = jnp.split(proj, points, axis=-1)
        out_a = _sliding_window_attention(
            qa.reshape(b, s, SWA_HEADS, HEAD_DIM),
            ka.reshape(b, s, SWA_KV_HEADS, HEAD_DIM),
            va.reshape(b, s, SWA_KV_HEADS, HEAD_DIM),
            swa_sinks[l], slopes_swa)
        lam_init = 0.8 - 0.6 * math.exp(-0.3 * l)
        lp = diff_lambda[l].astype(jnp.float32)
        lam = jnp.exp(jnp.sum(lp[0] * lp[1])) - jnp.exp(jnp.sum(lp[2] * lp[3])) + lam_init
        out_d = _diff_attention(
            qd.reshape(b, s, DIFF_HEADS, 2, HEAD_DIM),
            kd.reshape(b, s, DIFF_HEADS, 2, HEAD_DIM),
            vd.reshape(b, s, DIFF_HEADS, DIFF_VDIM),
            lam, lam_init, diff_subln_g[l], slopes_diff)
        merged = (jax.nn.sigmoid(ga) * (out_a @ w_branch_swa[l])
                  + jax.nn.sigmoid(gd) * (out_d @ w_branch_diff[l]))
        x = x + gate1 * (merged @ w_out[l])

        h2 = _rmsnorm(x, norm2_g[l]) * (1.0 + scale2) + shift2
        x = x + gate2 * _conv_ffn(h2, w_up[l], conv_w[l], conv_b[l], w_down[l])
    return _rmsnorm(x, final_g)
```

```python
import math
from contextlib import ExitStack
import numpy as np
import concourse.bass as bass
import concourse.mybir as mybir
from concourse.bass_utils import run_bass_kernel_spmd

F32 = mybir.dt.float32
BF16 = mybir.dt.bfloat16
AF = mybir.ActivationFunctionType
ALU = mybir.AluOpType
AX = mybir.AxisListType

P = 128
D = 1024
KC = 8
DFF = 2816
NJ = 22
INC = 4352
G = 512
EPS = 1e-6
ND = 36
NBUF = 6
USZ = 2048
NTMP = 7
NPT = 6

O_G1, O_G2, O_CW, O_CB, O_BADA, O_SINK, O_LAM, O_SUBG = 0, 8, 16, 82, 104, 152, 160, 416
LS = 417
C_ID, C_ONES, C_TRI, C_DB, C_EPS = 0, 128, 256, 384, 384 + 4 * ND
NCON = C_EPS + 1


STOP = {}


def lam_init_of(l):
    return 0.8 - 0.6 * math.exp(-0.3 * l)


class Sched:
    def __init__(self, nc):
        self.nc = nc
        self.eng = {}
        self.semh = {}
        for name, h in [("pe", nc.tensor), ("act", nc.scalar), ("dve", nc.vector),
                        ("pool", nc.gpsimd), ("sp", nc.sync)]:
            sem = nc.alloc_semaphore(name="s_" + name)
            self.eng[name] = dict(h=h, sem=sem, cnt=0, waited={})
            self.semh[name] = sem
        self.res = {}
        self.rings = {}
        self.n_wait = 0

    def add_ring(self, rname, n):
        lst = []
        for i in range(n):
            key = "d_%s_%d" % (rname, i)
            sem = self.nc.alloc_semaphore(name=key)
            self.semh[key] = sem
            lst.append(dict(key=key, sem=sem, val=0))
        self.rings[rname] = dict(lst=lst, rr=0)

    def _wait(self, ename, tok):
        key, val = tok
        e = self.eng[ename]
        if e["waited"].get(key, 0) >= val:
            return
        e["h"].wait_ge(self.semh[key], val)
        e["waited"][key] = val
        self.n_wait += 1

    def begin(self, ename, reads=(), writes=()):
        deps = []
        for r in reads:
            st = self.res.get(r)
            if st:
                deps += st["w"]
        for w in writes:
            st = self.res.get(w)
            if st:
                deps += st["w"] + list(st["r"].values())
        for tok in deps:
            if ename == "pe" and tok[0] == "pe":
                continue
            self._wait(ename, tok)

    def _record(self, tok, reads, writes):
        for r in reads:
            st = self.res.setdefault(r, dict(w=[], r={}))
            st["r"][tok[0]] = tok
        for w in writes:
            self.res[w] = dict(w=[tok], r={})

    def end(self, ename, inst, reads=(), writes=()):
        e = self.eng[ename]
        e["cnt"] += 1
        inst.then_inc(e["sem"], 1)
        tok = (ename, e["cnt"])
        self._record(tok, reads, writes)
        return tok

    def op(self, ename, fn, reads=(), writes=()):
        self.begin(ename, reads, writes)
        inst = fn(self.eng[ename]["h"])
        return self.end(ename, inst, reads, writes)

    def dma(self, q, ring, out, in_, reads=(), writes=()):
        rg = self.rings[ring]
        slot = rg["lst"][rg["rr"] % len(rg["lst"])]
        rg["rr"] += 1
        if slot["val"] > 0:
            self._wait(q, (slot["key"], slot["val"]))
        self.begin(q, reads, writes)
        slot["val"] += 16
        self.eng[q]["h"].dma_start(out=out, in_=in_).then_inc(slot["sem"], 16)
        tok = (slot["key"], slot["val"])
        self._record(tok, reads, writes)
        return tok

    def wait_all(self, ename):
        for st in self.res.values():
            for tok in st["w"] + list(st["r"].values()):
                self._wait(ename, tok)


def build(S, L):
    nc = bass.Bass("TRN2", target_bir_lowering=False)
    TG = S // G
    NBS = S // P
    NS = L * LS + 16 + 8
    O_C = L * LS
    O_FG = L * LS + 16
    dt = nc.dram_tensor
    x_d = dt("x", [S, D], F32, kind="ExternalInput").ap()
    smalls_d = dt("smalls", [P, NS], F32, kind="ExternalInput").ap()
    consts_d = dt("consts", [P, NCON], F32, kind="ExternalInput").ap()
    swab_d = dt("swab", [P, 2048], F32, kind="ExternalInput").ap()
    wada_d = dt("w_ada", [L, D, 6 * D], F32, kind="ExternalInput").ap()
    win_d = dt("w_in", [L, D, INC], F32, kind="ExternalInput").ap()
    wbs_d = dt("w_bs", [L, 512, D], F32, kind="ExternalInput").ap()
    wbd_d = dt("w_bd", [L, 512, D], F32, kind="ExternalInput").ap()
    wout_d = dt("w_out", [L, D, D], F32, kind="ExternalInput").ap()
    wup_d = dt("w_up", [L, D, 2 * DFF], F32, kind="ExternalInput").ap()
    wdn_d = dt("w_dn", [L, DFF, D], F32, kind="ExternalInput").ap()
    out_d = dt("out", [S, D], F32, kind="ExternalOutput").ap()
    dbg_d = dt("dbg", [P, 64], F32, kind="ExternalOutput").ap() if STOP.get("dbg") else None
    xT_d = dt("xT_scr", [D, S], F32, kind="Internal").ap()
    winb = [dt("winb%d" % l, [D, INC], BF16, kind="Internal").ap() for l in range(L)]
    wbsb = [dt("wbsb%d" % l, [512, D], BF16, kind="Internal").ap() for l in range(L)]
    wbdb = [dt("wbdb%d" % l, [512, D], BF16, kind="Internal").ap() for l in range(L)]
    woutb = [dt("woutb%d" % l, [D, D], BF16, kind="Internal").ap() for l in range(L)]
    wupb = [dt("wupb%d" % l, [D, 2 * DFF], BF16, kind="Internal").ap() for l in range(L)]
    wdnb = [dt("wdnb%d" % l, [DFF, D], BF16, kind="Internal").ap() for l in range(L)]

    es = ExitStack()
    sb = lambda name, shape, dtype: es.enter_context(nc.sbuf_tensor("sb_" + name, shape, dtype))
    xTb = [sb("xTbuf%d" % i, [P, KC * G], F32) for i in range(2)]
    hT = sb("hT", [P, KC * G], BF16)
    QM = sb("QM", [P, 8 * G], BF16)
    KdT = sb("KdT", [P, 4 * S], BF16)
    VdE = sb("VdE", [P, NBS * 4 * 130], BF16)
    KaT = sb("KaT", [P, 2 * 640], BF16)
    VaE = sb("VaE", [P, 5 * 132], BF16)
    oa_tok = sb("oa_tok", [P, 512], BF16)
    od_tok = [sb("od_tok%d" % i, [P, 128], BF16) for i in range(4)]
    oaT = sb("oaT", [P, 4 * G], BF16)
    odT = sb("odT", [P, 4 * G], BF16)
    ring = [sb("ring%d" % i, [P, USZ], BF16) for i in range(NBUF)]
    actT = [sb("actT%d" % i, [P, 4 * G], BF16) for i in range(2)]
    tmps = [sb("tmp%d" % i, [P, 520], F32) for i in range(NTMP)]
    PTs = [sb("PT%d" % i, [P, 512], BF16) for i in range(NPT)]
    sigs = [sb("sig%d" % i, [P, 512], BF16) for i in range(2)]
    adabig = sb("adabig", [P, KC * 384], BF16)
    consts = sb("consts", [P, NCON], F32)
    swaB = sb("swaB", [P, 2048], BF16)
    smalls = sb("smalls", [P, NS], F32)
    ident_b = sb("ident_b", [P, 128], BF16)
    ones_b = sb("ones_b", [P, 128], BF16)
    sqb = [sb("sqb%d" % i, [P, 512], BF16) for i in range(2)]
    tri_b = sb("tri_b", [P, 128], BF16)
    cs2 = sb("cs2", [P, 16], BF16)
    modv = sb("modv", [P, L * 48], F32)
    der = sb("der", [P, L * 16], F32)
    misc = sb("misc", [P, L * 16], F32)
    halo = sb("halo", [P, NJ * 2], F32)
    rsn = sb("rsn", [P, G], F32)
    ps = [es.enter_context(nc.psum_tensor("ps%d" % i, [P, 512], F32)) for i in range(8)]

    S_ = Sched(nc)
    S_.add_ring("w", NBUF)
    S_.add_ring("x", 16)
    S_.add_ring("cv", 8)
    S_.add_ring("ada", 4)
    S_.add_ring("m", 4)
    op = S_.op

    ident_f = consts[:, C_ID:C_ID + 128]
    ones_f = consts[:, C_ONES:C_ONES + 128]
    eps_ap = consts[:, C_EPS:C_EPS + 1]

    st = dict(gb=0, tmp=0, pt=0, sig=0, ring=0, od=0, ada=0, ev=0, sq=0, banks=list(range(STOP.get("nb", 8))))

    def gbank():
        pool_ = st["banks"]
        i = pool_[st["gb"] % len(pool_)]
        st["gb"] += 1
        return ps[i], "ps%d" % i

    def tmp():
        i = st["tmp"] % NTMP
        st["tmp"] += 1
        return tmps[i], "tmp%d" % i

    def sqbuf():
        i = st["sq"] % 2
        st["sq"] += 1
        return sqb[i], "sqb%d" % i

    def ptbuf():
        i = st["pt"] % NPT
        st["pt"] += 1
        return PTs[i], "PT%d" % i

    def sigbuf():
        i = st["sig"] % 2
        st["sig"] += 1
        return sigs[i], "sig%d" % i

    def evac(out, in_, reads, writes, eng=None):
        if eng is None:
            bk = [r for r in reads if r.startswith("ps")]
            eng = "act" if (bk and int(bk[0][2:]) % 2 == 0) else "dve"
        if eng == "act":
            return op("act", lambda e: e.activation(out=out, in_=in_, func=AF.Copy), reads, writes)
        return op(eng, lambda e: e.tensor_copy(out=out, in_=in_), reads, writes)

    def mm_group(out, pairs, reads, writes):
        S_.begin("pe", reads, writes)
        n = len(pairs)
        inst = None
        for i, (l_, r_) in enumerate(pairs):
            inst = nc.tensor.matmul(out, lhsT=l_, rhs=r_, start=(i == 0), stop=(i == n - 1))
        return S_.end("pe", inst, reads, writes)

    def load_unit(src, kcn, ncols, reads):
        i = st["ring"] % NBUF
        st["ring"] += 1
        dst = ring[i][:, 0:kcn * ncols].rearrange("p (k n) -> p k n", n=ncols)
        S_.dma("sp", "w", out=dst, in_=src, reads=reads, writes=["ring%d" % i])
        return ring[i], "ring%d" % i

    def wsrc(w, r0, nr, c0, ncols):
        return w[r0:r0 + nr, c0:c0 + ncols].rearrange("(kc p) n -> p kc n", p=P)

    S_.dma("sp", "m", out=consts[:], in_=consts_d[:, :], writes=["consts"])
    S_.dma("sp", "m", out=smalls[:], in_=smalls_d[:, :], writes=["smalls"])
    S_.dma("pool", "m", out=swaB[:], in_=swab_d[:, :], writes=["swaB"])
    op("dve", lambda e: e.tensor_copy(out=ident_b[:], in_=consts[:, C_ID:C_ID + 128]), ["consts"], ["ident_b"])
    op("dve", lambda e: e.tensor_copy(out=tri_b[:], in_=consts[:, C_TRI:C_TRI + 128]), ["consts"], ["tri_b"])
    op("dve", lambda e: e.tensor_copy(out=ones_b[:], in_=consts[:, C_ONES:C_ONES + 128]), ["consts"], ["ones_b"])
    op("pool", lambda e: e.memset(VdE[:], 1.0), [], ["VdE%d" % b for b in range(NBS)])
    op("pool", lambda e: e.memset(VaE[:], 1.0), [], ["VaEprev", "VaEcur"])
    op("act", lambda e: e.activation(out=cs2[:, 0:16], in_=smalls[:, O_C:O_C + 16], func=AF.Silu), ["smalls"], ["cs2"])

    def emit_convert(l):
        for (dst, src, rn) in [(winb[l], win_d[l], "winb"), (wbsb[l], wbs_d[l], "wbsb"), (wbdb[l], wbd_d[l], "wbdb"),
                               (woutb[l], wout_d[l], "woutb"), (wupb[l], wup_d[l], "wupb"), (wdnb[l], wdn_d[l], "wdnb")]:
            S_.dma("pool", "cv", out=dst[:, :], in_=src, writes=["%s%d" % (rn, l)])

    def emit_mod_load(l, cis):
        cis = list(cis)
        n = len(cis)
        if n == 0:
            return
        S_.dma("pool", "ada", out=adabig[:, 0:KC * n * 128].rearrange("p (k n) -> p k n", n=n * 128),
               in_=wsrc(wada_d[l], 0, D, cis[0] * 128, n * 128), writes=["adabig"])

    def emit_mod_mm(l, cis):
        cis = list(cis)
        n = len(cis)
        for ii, ci in enumerate(cis):
            bank, bn = gbank()
            mm_group(bank[:, 0:2], [(adabig[:, kc * n * 128 + ii * 128:kc * n * 128 + (ii + 1) * 128], cs2[:, 2 * kc:2 * kc + 2]) for kc in range(KC)],
                     ["adabig", "cs2"], [bn])
            op("dve", lambda e, ci=ci, bank=bank: e.tensor_tensor(
                out=modv[:, l * 48 + ci:l * 48 + ci + 1], in0=bank[:, 0:1],
                in1=smalls[:, l * LS + O_BADA + ci:l * LS + O_BADA + ci + 1], op=ALU.add),
               [bn, "smalls"], ["modv%d" % l])

    def emit_mod(l, cis):
        cis = list(cis)
        for k in range(0, len(cis), 3):
            emit_mod_load(l, cis[k:k + 3])
            emit_mod_mm(l, cis[k:k + 3])

    def emit_der(l):
        for (o0, m0, g0) in [(0, 8, O_G1), (8, 32, O_G2)]:
            op("dve", lambda e, o0=o0, m0=m0, g0=g0: e.scalar_tensor_tensor(
                out=der[:, l * 16 + o0:l * 16 + o0 + 8], in0=modv[:, l * 48 + m0:l * 48 + m0 + 8], scalar=1.0,
                in1=smalls[:, l * LS + g0:l * LS + g0 + 8], op0=ALU.add, op1=ALU.mult),
               ["modv%d" % l, "smalls"], ["der%d" % l])

    def emit_misc(l):
        b = l * 16
        li = lam_init_of(l)
        lo = l * LS + O_LAM
        op("act", lambda e: e.activation(out=misc[:, b:b + 8], in_=smalls[:, l * LS + O_SINK:l * LS + O_SINK + 8], func=AF.Exp),
           ["smalls"], ["misc"])
        t, tn = tmp()
        for k in range(2):
            op("dve", lambda e, k=k: e.tensor_tensor(out=t[:, k * 64:(k + 1) * 64], in0=smalls[:, lo + k * 128:lo + k * 128 + 64],
                                                     in1=smalls[:, lo + k * 128 + 64:lo + k * 128 + 128], op=ALU.mult), ["smalls"], [tn])
            op("dve", lambda e, k=k: e.tensor_reduce(out=misc[:, b + 10 + k:b + 11 + k], in_=t[:, k * 64:(k + 1) * 64], axis=AX.X, op=ALU.add),
               [tn], ["misc"])
        op("act", lambda e: e.activation(out=misc[:, b + 12:b + 14], in_=misc[:, b + 10:b + 12], func=AF.Exp), ["misc"], ["misc"])
        op("dve", lambda e: e.tensor_tensor(out=misc[:, b + 8:b + 9], in0=misc[:, b + 13:b + 14], in1=misc[:, b + 12:b + 13], op=ALU.subtract),
           ["misc"], ["misc"])
        op("dve", lambda e: e.tensor_scalar(out=misc[:, b + 8:b + 9], in0=misc[:, b + 8:b + 9], scalar1=-li, scalar2=None, op0=ALU.add),
           ["misc"], ["misc"])
        op("dve", lambda e: e.tensor_scalar(out=misc[:, b + 9:b + 10], in0=smalls[:, l * LS + O_SUBG:l * LS + O_SUBG + 1],
                                            scalar1=(1.0 - li), scalar2=None, op0=ALU.mult), ["smalls", "misc"], ["misc"])

    emit_convert(0)
    for l in range(L):
        emit_misc(l)
    emit_mod(0, range(48))
    emit_der(0)

    def norm_stats(xT, XN, rs, rn):
        bank, bn = gbank()
        for kc in range(KC):
            sq, sn = (sqbuf() if STOP.get("F", 1) else tmp())
            op("act", lambda e, kc=kc, sq=sq: e.activation(out=sq[:, 0:G], in_=xT[:, kc * G:(kc + 1) * G], func=AF.Square),
               [XN(kc)], [sn])
            op("pe", lambda e, kc=kc, sq=sq: e.matmul(bank[:, 0:G], lhsT=(ones_b[:] if STOP.get("F", 1) else ones_f), rhs=sq[:, 0:G], start=(kc == 0), stop=(kc == KC - 1)),
               [sn, "ones_b", "consts"], [bn])
        op("act", lambda e: e.activation(out=rs[:, 0:G], in_=bank[:, 0:G], func=AF.Sqrt, bias=eps_ap, scale=1.0), [bn, "consts"], [rn])
        op("dve", lambda e: e.reciprocal(out=rs[:, 0:G], in_=rs[:, 0:G]), [rn], [rn])

    def norm_to_hT(l, which, xT, XN, pre_rs=None):
        if pre_rs is None:
            rs, rn = tmp()
            norm_stats(xT, XN, rs, rn)
        else:
            rs, rn = pre_rs
        so = l * 16 + (0 if which == 1 else 8)
        sho = l * 48 + (0 if which == 1 else 24)
        for kc in range(KC):
            t, tn = tmp()
            if tn == rn:
                t, tn = tmp()
            op("dve", lambda e, kc=kc, t=t: e.scalar_tensor_tensor(out=t[:, 0:G], in0=xT[:, kc * G:(kc + 1) * G],
                                                                  scalar=der[:, so + kc:so + kc + 1], in1=rs[:, 0:G],
                                                                  op0=ALU.mult, op1=ALU.mult),
               [XN(kc), rn, "der%d" % l], [tn])
            op("act", lambda e, kc=kc, t=t: e.activation(out=hT[:, kc * G:(kc + 1) * G], in_=t[:, 0:G], func=AF.Identity,
                                                         bias=modv[:, sho + kc:sho + kc + 1], scale=1.0),
               [tn, "modv%d" % l], ["hT%d" % kc])

    HT_ALL = ["hT%d" % kc for kc in range(KC)]

    def prepare_x(l, g, p):
        xT = xTb[p]
        XN = lambda kc: "xT%d_%d" % (p, kc)
        t0 = g * G
        if l == 0:
            for b in range(4):
                for hf in range(2):
                    tb, tn = tmp()
                    S_.dma("pool", "x", out=tb[:, 0:512], in_=x_d[t0 + b * 128:t0 + (b + 1) * 128, hf * 512:(hf + 1) * 512], writes=[tn])
                    bank, bn = gbank()
                    S_.begin("pe", [tn, "consts"], [bn])
                    inst = None
                    for cc in range(4):
                        inst = nc.tensor.transpose(out=bank[:, cc * 128:(cc + 1) * 128], in_=tb[:, cc * 128:(cc + 1) * 128], identity=ident_f)
                    S_.end("pe", inst, [tn, "consts"], [bn])
                    outv = xT[:, hf * 4 * G:(hf + 1) * 4 * G].rearrange("p (c t) -> p c t", t=G)[:, :, b * 128:(b + 1) * 128]
                    evac(outv, bank[:, 0:512].rearrange("p (c t) -> p c t", t=128), [bn], [XN(hf * 4 + cc) for cc in range(4)])
        else:
            for kc in range(KC):
                S_.dma("pool", "x", out=xT[:, kc * G:(kc + 1) * G], in_=xT_d[kc * P:(kc + 1) * P, t0:t0 + G],
                       reads=["xTd%d_%d" % (g, kc)], writes=[XN(kc)])

    def layer_group(l, g, mid_hook=None):
        t0 = g * G
        last_layer = (l == L - 1)
        p = (l * TG + g) % 2
        xT = xTb[p]
        XN = lambda kc: "xT%d_%d" % (p, kc)
        if STOP.get('s', 99) == 1:
            return
        norm_to_hT(l, 1, xT, XN, pre_rs=(rsn, "rsn"))
        if STOP.get('s', 99) == 2:
            return
        wl = "winb%d" % l
        for uh in range(2):
            u, un = load_unit(wsrc(winb[l], 0, D, 1792 + uh * 256, 256), KC, 256, [wl])
            for bp in range(2):
                bank, bn = gbank()
                for bb in range(2):
                    b = bp * 2 + bb
                    mm_group(bank[:, bb * 256:(bb + 1) * 256],
                             [(hT[:, kc * G + b * 128:kc * G + (b + 1) * 128], u[:, kc * 256:(kc + 1) * 256]) for kc in range(KC)],
                             HT_ALL + [un], [bn])
                if STOP.get('s', 99) == 211:
                    return
                for bb in range(2):
                    b = bp * 2 + bb
                    blk = 4 * g + b
                    o0 = (blk * 4 + 2 * uh) * 130
                    for hh in range(2):
                        evac(VdE[:, o0 + hh * 130:o0 + hh * 130 + 128], bank[:, bb * 256 + hh * 128:bb * 256 + (hh + 1) * 128], [bn], ["VdE%d" % blk])
        if STOP.get('s', 99) == 21:
            return
        u, un = load_unit(wsrc(winb[l], 0, D, 512, 256), KC, 256, [wl])
        i = st["ring"] % NBUF
        st["ring"] += 1
        usw, uswn = ring[i], "ring%d" % i
        uswv = usw[:, 0:1024].rearrange("p (k n) -> p k n", n=128)
        S_.dma("sp", "w", out=uswv[:, :, 0:64], in_=wsrc(winb[l], 0, D, 576, 64), reads=[wl], writes=[uswn])
        S_.dma("sp", "w", out=uswv[:, :, 64:128], in_=wsrc(winb[l], 0, D, 512, 64), reads=[wl, uswn], writes=[uswn])
        for r in range(2):
            bank, bn = gbank()
            if r == 0:
                prs = [(u[:, kc * 256:kc * 256 + 128], hT[:, kc * G:(kc + 1) * G]) for kc in range(KC)]
            else:
                prs = [(usw[:, kc * 128:(kc + 1) * 128], hT[:, kc * G:(kc + 1) * G]) for kc in range(KC)]
            mm_group(bank[:, 0:G], prs, HT_ALL + [un, uswn], [bn])
            evac(KaT[:, r * 640 + 128:r * 640 + 640], bank[:, 0:G], [bn], ["KaTcur"])
        if STOP.get('s', 99) == 22:
            return
        bank, bn = gbank()
        for b in range(4):
            mm_group(bank[:, b * 128:(b + 1) * 128],
                     [(hT[:, kc * G + b * 128:kc * G + (b + 1) * 128], u[:, kc * 256 + 128:kc * 256 + 256]) for kc in range(KC)],
                     HT_ALL + [un], [bn])
        for b in range(4):
            for k in range(2):
                evac(VaE[:, (b + 1) * 132 + k * 66:(b + 1) * 132 + k * 66 + 64], bank[:, b * 128 + k * 64:b * 128 + (k + 1) * 64], [bn], ["VaEcur"])
        if STOP.get('s', 99) == 23:
            return
        for (c_base, kind) in [(1280, "kd"), (768, "qd"), (0, "qa")]:
            for uh in range(2):
                u, un = load_unit(wsrc(winb[l], 0, D, c_base + uh * 256, 256), KC, 256, [wl])
                for cc in range(2):
                    ch = uh * 2 + cc
                    bank, bn = gbank()
                    mm_group(bank[:, 0:G], [(u[:, kc * 256 + cc * 128:kc * 256 + (cc + 1) * 128], hT[:, kc * G:(kc + 1) * G]) for kc in range(KC)],
                             HT_ALL + [un], [bn])
                    if kind == "kd":
                        evac(KdT[:, ch * S + t0:ch * S + t0 + G], bank[:, 0:G], [bn], ["KdT%d_%d" % (ch, g)])
                    elif kind == "qd":
                        evac(QM[:, ch * G:(ch + 1) * G], bank[:, 0:G], [bn], ["QM%d" % ch])
                    else:
                        evac(QM[:, (4 + ch) * G:(5 + ch) * G], bank[:, 0:G], [bn], ["QM%d" % (4 + ch)])
        if STOP.get('s', 99) == 3:
            return
        g2 = (g + 1) % TG
        l2 = l if g + 1 < TG else l + 1
        if l2 < L:
            prepare_x(l2, g2, 1 - p)
        st["banks"] = [7]
        QA = ["QM%d" % c for c in range(4, 8)]
        for b in range(4):
            blk = 4 * g + b
            pts = {}
            for which in (0, 1):
                if which == 0 and blk == 0:
                    continue
                if which == 1:
                    kc0, kres = 128 + b * 128, ["KaTcur"]
                elif b == 0:
                    kc0, kres = 0, ["KaTprev"]
                else:
                    kc0, kres = 128 + (b - 1) * 128, ["KaTcur"]
                for half in range(2):
                    bi = half + 2 * which
                    bank, bn = ps[bi], "ps%d" % bi
                    S_.begin("pe", kres + QA, [bn])
                    inst = None
                    for c in range(4):
                        h = 2 * c + half
                        kv = h // 4
                        inst = nc.tensor.matmul(bank[:, c * 128:(c + 1) * 128],
                                                lhsT=KaT[half * 64:(half + 1) * 64, (0 if kv == half else 1) * 640 + kc0:(0 if kv == half else 1) * 640 + kc0 + 128],
                                                rhs=QM[half * 64:(half + 1) * 64, (4 + c) * G + b * 128:(4 + c) * G + (b + 1) * 128],
                                                start=True, stop=True)
                    S_.end("pe", inst, kres + QA, [bn])
                    sc, scn = tmp()
                    bo = (which * 2 + half) * 512
                    op("dve", lambda e, sc=sc, bank=bank, bo=bo: e.scalar_tensor_tensor(
                        out=sc[:, 0:512], in0=bank[:, 0:512], scalar=0.125, in1=swaB[:, bo:bo + 512], op0=ALU.mult, op1=ALU.add),
                       [bn, "swaB"], [scn])
                    pt, ptn = ptbuf()
                    op("act", lambda e, sc=sc, pt=pt: e.activation(out=pt[:, 0:512], in_=sc[:, 0:512], func=AF.Exp), [scn], [ptn])
                    pts[(which, half)] = (pt, ptn)
            if STOP.get('s', 99) == 31:
                return
            whichs = [w_ for w_ in (0, 1) if (w_, 0) in pts]
            rd = [pts[k][1] for k in pts] + ["VaEprev", "VaEcur"]
            S_.begin("pe", rd, ["ps4", "ps5"])
            inst = None
            for h in range(8):
                half, c, kv = h % 2, h // 2, h // 4
                reg = ps[4 + h // 4][:, (h % 4) * 65:(h % 4) * 65 + 65]
                for i, w_ in enumerate(whichs):
                    vb = b + w_
                    inst = nc.tensor.matmul(reg, lhsT=pts[(w_, half)][0][:, c * 128:(c + 1) * 128],
                                            rhs=VaE[:, vb * 132 + kv * 66:vb * 132 + kv * 66 + 65],
                                            start=(i == 0), stop=(i == len(whichs) - 1))
            S_.end("pe", inst, rd, ["ps4", "ps5"])
            if STOP.get('s', 99) == 32:
                return
            sm, smn = tmp()
            S_.begin("dve", ["ps4", "ps5", "misc"], [smn])
            inst = None
            for h in range(8):
                inst = nc.vector.tensor_tensor(out=sm[:, h:h + 1], in0=ps[4 + h // 4][:, (h % 4) * 65 + 64:(h % 4) * 65 + 65],
                                               in1=misc[:, l * 16 + h:l * 16 + h + 1], op=ALU.add)
            S_.end("dve", inst, ["ps4", "ps5", "misc"], [smn])
            op("dve", lambda e, sm=sm: e.reciprocal(out=sm[:, 8:16], in_=sm[:, 0:8]), [smn], [smn])
            S_.begin("dve", [smn, "ps4", "ps5"], ["oa_tok"])
            inst = None
            for h in range(8):
                inst = nc.vector.tensor_scalar(out=oa_tok[:, h * 64:(h + 1) * 64], in0=ps[4 + h // 4][:, (h % 4) * 65:(h % 4) * 65 + 64],
                                               scalar1=sm[:, 8 + h:9 + h], scalar2=None, op0=ALU.mult)
            S_.end("dve", inst, [smn, "ps4", "ps5"], ["oa_tok"])
            if STOP.get('s', 99) == 33:
                return
            tbank, tbn = ps[6], "ps6"
            S_.begin("pe", ["oa_tok", "ident_b"], [tbn])
            inst = None
            for c in range(4):
                inst = nc.tensor.matmul(tbank[:, c * 128:(c + 1) * 128], lhsT=oa_tok[:, c * 128:(c + 1) * 128], rhs=ident_b[:], start=True, stop=True)
            S_.end("pe", inst, ["oa_tok", "ident_b"], [tbn])
            if STOP.get('s', 99) == 335:
                return
            for c in range(4):
                if STOP.get('v') == 1:
                    evac(PTs[c][:, 0:128], tbank[:, c * 128:(c + 1) * 128], [tbn], ["PT%d" % c])
                elif STOP.get('v') == 2:
                    evac(oaT[:, c * G + b * 128:c * G + (b + 1) * 128], tbank[:, c * 128:(c + 1) * 128], [tbn], ["oaT%d" % c], eng="dve")
                elif STOP.get('v') == 3:
                    evac(oaT[:, c * G + b * 128:c * G + (b + 1) * 128], tbank[:, c * 128:(c + 1) * 128], [tbn], ["oaT%d" % c], eng="act")
                else:
                    evac(oaT[:, c * G + b * 128:c * G + (b + 1) * 128], tbank[:, c * 128:(c + 1) * 128], [tbn], ["oaT%d" % c], eng="dve")
            if STOP.get('s', 99) == 34:
                return
        if g < TG - 1:
            for r in range(2):
                op("pool", lambda e, r=r: e.tensor_copy(out=KaT[:, r * 640:r * 640 + 128], in_=KaT[:, r * 640 + 512:r * 640 + 640]),
                   ["KaTcur"], ["KaTprev"])
            op("pool", lambda e: e.tensor_copy(out=VaE[:, 0:132], in_=VaE[:, 528:660]), ["VaEcur"], ["VaEprev"])
        if STOP.get('s', 99) == 4:
            return
        nkb = 4 * g + 4
        for h in range(4):
            W = 1 if h == 0 else 4
            slope_h = 2.0 ** (-2.0 * (h + 1))
            dskip = int(math.ceil((200.0 / slope_h + 127.0) / 128.0))
            kb0 = max(0, 4 * g - dskip + 1)

            def region(m, j):
                r = m * 4 + j
                return ps[4 + r // 3][:, (r % 3) * 130:(r % 3) * 130 + 129], "ps%d" % (4 + r // 3)

            def scores(kb):
                j0 = max(0, kb - 4 * g)
                c0 = j0 * 128
                n = 512 - c0
                res = {}
                for m in range(2):
                    bi = m + 2 * (kb % 2)
                    bank, bn = ps[bi], "ps%d" % bi
                    op("pe", lambda e, m=m, bank=bank: e.matmul(
                        bank[:, 0:n], lhsT=KdT[m * 64:(m + 1) * 64, h * S + kb * 128:h * S + (kb + 1) * 128],
                        rhs=QM[m * 64:(m + 1) * 64, h * G + c0:(h + 1) * G], start=True, stop=True),
                       ["KdT%d_%d" % (h, kb // 4), "QM%d" % h], [bn])
                    pt, ptn = ptbuf()
                    S_.begin("act", [bn, "consts"], [ptn])
                    inst = None
                    if W == 4:
                        di = C_DB + h * ND + (4 * g - kb + 3)
                        inst = nc.scalar.activation(out=pt[:, 0:n], in_=bank[:, 0:n], func=AF.Exp, bias=consts[:, di:di + 1], scale=0.125)
                    else:
                        for j in range(j0, 4):
                            di = C_DB + h * ND + (4 * g + j - kb + 3)
                            a0 = j * 128 - c0
                            inst = nc.scalar.activation(out=pt[:, a0:a0 + 128], in_=bank[:, a0:a0 + 128], func=AF.Exp,
                                                        bias=consts[:, di:di + 1], scale=0.125)
                    S_.end("act", inst, [bn, "consts"], [ptn])
                    if kb >= 4 * g:
                        op("pool", lambda e, pt=pt: e.tensor_tensor(out=pt[:, 0:128], in0=pt[:, 0:128], in1=tri_b[:], op=ALU.mult),
                           [ptn, "tri_b"], [ptn])
                    res[m] = (pt, ptn, c0, j0)
                return res

            def pv(kb, sc):
                for m in range(2):
                    pt, ptn, c0, j0 = sc[m]
                    wr = sorted(set(region(m, j)[1] for j in range(j0, 4)))
                    S_.begin("pe", [ptn, "VdE%d" % kb], wr)
                    inst = None
                    for j in range(j0, 4):
                        reg, _ = region(m, j)
                        a0 = j * 128 - c0
                        vo = (kb * 4 + h) * 130
                        inst = nc.tensor.matmul(reg, lhsT=pt[:, a0:a0 + 128], rhs=VdE[:, vo:vo + 129],
                                                start=(kb == kb0 and (m, j) in ((0, 0), (0, 3), (1, 2))), stop=(kb == 4 * g + j))
                    S_.end("pe", inst, [ptn, "VdE%d" % kb], wr)

            prev = scores(kb0)
            for kb in range(kb0 + 1, nkb):
                cur = scores(kb)
                pv(kb - 1, prev)
                prev = cur
            pv(nkb - 1, prev)
            mo = l * 16
            R0 = [region(0, j) for j in range(4)]
            R1 = [region(1, j) for j in range(4)]
            SM = [tmp() for j in range(4)]
            J4 = range(4)
            for j in J4:
                sm, smn = SM[j]
                op("dve", lambda e, sm=sm, r0=R0[j][0]: e.reciprocal(out=sm[:, 0:1], in_=r0[:, 128:129]), [R0[j][1]], [smn])
            for j in J4:
                sm, smn = SM[j]
                op("dve", lambda e, sm=sm, r1=R1[j][0]: e.reciprocal(out=sm[:, 1:2], in_=r1[:, 128:129]), [R1[j][1], smn], [smn])
            for j in J4:
                sm, smn = SM[j]
                op("dve", lambda e, sm=sm: e.tensor_scalar(out=sm[:, 2:3], in0=sm[:, 1:2], scalar1=misc[:, mo + 8:mo + 9], scalar2=None, op0=ALU.mult),
                   [smn, "misc"], [smn])
            for j in J4:
                sm, smn = SM[j]
                op("dve", lambda e, sm=sm, r0=R0[j][0]: e.tensor_scalar(out=sm[:, 128:256], in0=r0[:, 0:128], scalar1=sm[:, 0:1], scalar2=None, op0=ALU.mult),
                   [R0[j][1], smn], [smn])
            for j in J4:
                sm, smn = SM[j]
                op("dve", lambda e, sm=sm, r1=R1[j][0]: e.scalar_tensor_tensor(out=sm[:, 128:256], in0=r1[:, 0:128], scalar=sm[:, 2:3], in1=sm[:, 128:256],
                                                                            op0=ALU.mult, op1=ALU.add), [R1[j][1], smn], [smn])
            for j in J4:
                sm, smn = SM[j]
                op("act", lambda e, sm=sm: e.activation(out=sm[:, 256:384], in_=sm[:, 128:256], func=AF.Square, accum_out=sm[:, 3:4]), [smn], [smn])
            for j in J4:
                sm, smn = SM[j]
                op("act", lambda e, sm=sm: e.activation(out=sm[:, 4:5], in_=sm[:, 3:4], func=AF.Sqrt, bias=eps_ap, scale=1.0 / 128.0),
                   [smn, "consts"], [smn])
            for j in J4:
                sm, smn = SM[j]
                op("dve", lambda e, sm=sm: e.reciprocal(out=sm[:, 5:6], in_=sm[:, 4:5]), [smn], [smn])
            for j in J4:
                sm, smn = SM[j]
                op("dve", lambda e, sm=sm, j=j: e.tensor_scalar(out=od_tok[j][:], in0=sm[:, 128:256], scalar1=sm[:, 5:6], scalar2=None, op0=ALU.mult),
                   [smn], ["od_tok%d" % j])
            tbank, tbn = gbank()
            rdl = ["od_tok%d" % j for j in J4] + ["ident_b"]
            S_.begin("pe", rdl, [tbn])
            inst = None
            for j in J4:
                inst = nc.tensor.matmul(tbank[:, j * 128:(j + 1) * 128], lhsT=od_tok[j][:], rhs=ident_b[:], start=True, stop=True)
            S_.end("pe", inst, rdl, [tbn])
            op("act", lambda e, tbank=tbank: e.activation(out=odT[:, h * G:(h + 1) * G], in_=tbank[:, 0:G], func=AF.Copy,
                                                        scale=misc[:, mo + 9:mo + 10]), [tbn, "misc"], ["odT%d" % h])
        if STOP.get('s', 99) == 5:
            return
        st["banks"] = list(range(STOP.get("nb", 8)))
        if mid_hook is not None:
            mid_hook()
        if l2 < L:
            norm_stats(xTb[1 - p], (lambda kc: "xT%d_%d" % (1 - p, kc)), rsn, "rsn")
        OA = ["oaT%d" % c for c in range(4)]
        OD = ["odT%d" % c for c in range(4)]
        for op2 in range(4):
            i = st["ring"] % NBUF
            st["ring"] += 1
            ub, ubn = ring[i], "ring%d" % i
            S_.dma("sp", "w", out=ub[:, 0:1024].rearrange("p (k n) -> p k n", n=256), in_=wsrc(wbsb[l], 0, 512, op2 * 256, 256),
                   reads=["wbsb%d" % l], writes=[ubn])
            S_.dma("sp", "w", out=ub[:, 1024:2048].rearrange("p (k n) -> p k n", n=256), in_=wsrc(wbdb[l], 0, 512, op2 * 256, 256),
                   reads=["wbdb%d" % l, ubn], writes=[ubn])
            uga, ugan = load_unit(wsrc(winb[l], 0, D, 2304 + op2 * 256, 256), KC, 256, [wl])
            ugd, ugdn = load_unit(wsrc(winb[l], 0, D, 3328 + op2 * 256, 256), KC, 256, [wl])
            for cc in range(2):
                oc = op2 * 2 + cc
                cs_ = slice(cc * 128, (cc + 1) * 128)
                bga, bgan = gbank()
                mm_group(bga[:, 0:G], [(uga[:, kc * 256 + cc * 128:kc * 256 + (cc + 1) * 128], hT[:, kc * G:(kc + 1) * G]) for kc in range(KC)],
                         HT_ALL + [ugan], [bgan])
                sga, sgan = sigbuf()
                op("act", lambda e, sga=sga, bga=bga: e.activation(out=sga[:, 0:G], in_=bga[:, 0:G], func=AF.Sigmoid), [bgan], [sgan])
                ba, ban = gbank()
                mm_group(ba[:, 0:G], [(ub[:, kc * 256 + cc * 128:kc * 256 + (cc + 1) * 128], oaT[:, kc * G:(kc + 1) * G]) for kc in range(4)],
                         OA + [ubn], [ban])
                t1, t1n = tmp()
                op("dve", lambda e, t1=t1, ba=ba, sga=sga: e.tensor_tensor(out=t1[:, 0:G], in0=ba[:, 0:G], in1=sga[:, 0:G], op=ALU.mult),
                   [ban, sgan], [t1n])
                bgd, bgdn = gbank()
                mm_group(bgd[:, 0:G], [(ugd[:, kc * 256 + cc * 128:kc * 256 + (cc + 1) * 128], hT[:, kc * G:(kc + 1) * G]) for kc in range(KC)],
                         HT_ALL + [ugdn], [bgdn])
                sgd, sgdn = sigbuf()
                op("act", lambda e, sgd=sgd, bgd=bgd: e.activation(out=sgd[:, 0:G], in_=bgd[:, 0:G], func=AF.Sigmoid), [bgdn], [sgdn])
                bd, bdn = gbank()
                mm_group(bd[:, 0:G], [(ub[:, 1024 + kc * 256 + cc * 128:1024 + kc * 256 + (cc + 1) * 128], odT[:, kc * G:(kc + 1) * G]) for kc in range(4)],
                         OD + [ubn], [bdn])
                t2, t2n = tmp()
                op("dve", lambda e, t2=t2, bd=bd, sgd=sgd: e.tensor_tensor(out=t2[:, 0:G], in0=bd[:, 0:G], in1=sgd[:, 0:G], op=ALU.mult),
                   [bdn, sgdn], [t2n])
                op("pool", lambda e, oc=oc, t1=t1, t2=t2: e.tensor_tensor(out=QM[:, oc * G:(oc + 1) * G], in0=t1[:, 0:G], in1=t2[:, 0:G], op=ALU.add),
                   [t1n, t2n], ["QM%d" % oc])
        if STOP.get('s', 99) == 6:
            return
        MG = ["QM%d" % c for c in range(8)]
        for op2 in range(4):
            u, un = load_unit(wsrc(woutb[l], 0, D, op2 * 256, 256), KC, 256, ["woutb%d" % l])
            for cc in range(2):
                oc = op2 * 2 + cc
                bank, bn = gbank()
                mm_group(bank[:, 0:G], [(u[:, kc * 256 + cc * 128:kc * 256 + (cc + 1) * 128], QM[:, kc * G:(kc + 1) * G]) for kc in range(KC)],
                         MG + [un], [bn])
                go = l * 48 + 16 + oc
                op("dve", lambda e, oc=oc, bank=bank, go=go: e.scalar_tensor_tensor(
                    out=xT[:, oc * G:(oc + 1) * G], in0=bank[:, 0:G], scalar=modv[:, go:go + 1], in1=xT[:, oc * G:(oc + 1) * G],
                    op0=ALU.mult, op1=ALU.add), [bn, "modv%d" % l, XN(oc)], [XN(oc)])
        if STOP.get('s', 99) == 7:
            return
        norm_to_hT(l, 2, xT, XN)
        if STOP.get('s', 99) == 8:
            return
        if g == 0:
            op("pool", lambda e: e.memset(halo[:], 0.0), [], ["halo"])
        cwo = l * LS + O_CW
        cbo = l * LS + O_CB
        def ffn_up(q):
            npair = 2 if q < 5 else 1
            ai = q % 2
            at, atn = actT[ai], "actT%d" % ai
            for pr in range(npair):
                j0 = q * 4 + pr * 2
                ua_u, ua_n = load_unit(wsrc(wupb[l], 0, D, j0 * 128, 256), KC, 256, ["wupb%d" % l])
                ug_u, ug_n = load_unit(wsrc(wupb[l], 0, D, DFF + j0 * 128, 256), KC, 256, ["wupb%d" % l])
                for cc in range(2):
                    j = j0 + cc
                    jj = pr * 2 + cc
                    ba, ban = gbank()
                    mm_group(ba[:, 0:G], [(ua_u[:, kc * 256 + cc * 128:kc * 256 + (cc + 1) * 128], hT[:, kc * G:(kc + 1) * G]) for kc in range(KC)],
                             HT_ALL + [ua_n], [ban])
                    ua, uan = tmp()
                    op("pool", lambda e, ua=ua, j=j: e.tensor_copy(out=ua[:, 0:2], in_=halo[:, 2 * j:2 * j + 2]), ["halo"], [uan])
                    op("act", lambda e, ua=ua, ba=ba: e.activation(out=ua[:, 2:2 + G], in_=ba[:, 0:G], func=AF.Copy), [ban, uan], [uan])
                    op("pool", lambda e, ua=ua, j=j: e.tensor_copy(out=halo[:, 2 * j:2 * j + 2], in_=ua[:, G:G + 2]), [uan], ["halo"])
                    cv, cvn = tmp()
                    op("dve", lambda e, ua=ua, cv=cv, j=j: e.tensor_scalar(
                        out=cv[:, 0:G], in0=ua[:, 2:2 + G], scalar1=smalls[:, cwo + 2 * NJ + j:cwo + 2 * NJ + j + 1],
                        scalar2=smalls[:, cbo + j:cbo + j + 1], op0=ALU.mult, op1=ALU.add), [uan, "smalls"], [cvn])
                    op("dve", lambda e, ua=ua, cv=cv, j=j: e.scalar_tensor_tensor(
                        out=cv[:, 0:G], in0=ua[:, 1:1 + G], scalar=smalls[:, cwo + NJ + j:cwo + NJ + j + 1], in1=cv[:, 0:G],
                        op0=ALU.mult, op1=ALU.add), [uan, "smalls", cvn], [cvn])
                    op("dve", lambda e, ua=ua, cv=cv, j=j: e.scalar_tensor_tensor(
                        out=cv[:, 0:G], in0=ua[:, 0:G], scalar=smalls[:, cwo + j:cwo + j + 1], in1=cv[:, 0:G],
                        op0=ALU.mult, op1=ALU.add), [uan, "smalls", cvn], [cvn])
                    op("act", lambda e, cv=cv: e.activation(out=cv[:, 0:G], in_=cv[:, 0:G], func=AF.Gelu), [cvn], [cvn])
                    bg, bgn = gbank()
                    mm_group(bg[:, 0:G], [(ug_u[:, kc * 256 + cc * 128:kc * 256 + (cc + 1) * 128], hT[:, kc * G:(kc + 1) * G]) for kc in range(KC)],
                             HT_ALL + [ug_n], [bgn])
                    op("dve", lambda e, cv=cv, bg=bg, jj=jj, at=at: e.tensor_tensor(out=at[:, jj * G:(jj + 1) * G], in0=bg[:, 0:G], in1=cv[:, 0:G], op=ALU.mult),
                       [bgn, cvn], [atn])

        def ffn_down(q):
            npair = 2 if q < 5 else 1
            ai = q % 2
            at, atn = actT[ai], "actT%d" % ai
            nkc = npair * 2
            dus = []
            for pr in range(npair):
                j0 = q * 4 + pr * 2
                dus.append(load_unit(wsrc(wdnb[l], j0 * 128, 256, 0, D), 2, D, ["wdnb%d" % l]))
            for oc in range(KC):
                bank, bn = gbank()
                pairs = []
                for jj in range(nkc):
                    du = dus[jj // 2][0]
                    pairs.append((du[:, (jj % 2) * D + oc * 128:(jj % 2) * D + (oc + 1) * 128], at[:, jj * G:(jj + 1) * G]))
                mm_group(bank[:, 0:G], pairs, [atn] + [d_[1] for d_ in dus], [bn])
                go = l * 48 + 40 + oc
                op("dve", lambda e, oc=oc, bank=bank, go=go: e.scalar_tensor_tensor(
                    out=xT[:, oc * G:(oc + 1) * G], in0=bank[:, 0:G], scalar=modv[:, go:go + 1], in1=xT[:, oc * G:(oc + 1) * G],
                    op0=ALU.mult, op1=ALU.add), [bn, "modv%d" % l, XN(oc)], [XN(oc)])

        ffn_up(0)
        for q in range(6):
            if q + 1 < 6:
                ffn_up(q + 1)
            ffn_down(q)
        if STOP.get('s', 99) == 9:
            return
        if not last_layer:
            for kc in range(KC):
                S_.dma("pool", "x", out=xT_d[kc * P:(kc + 1) * P, t0:t0 + G], in_=xT[:, kc * G:(kc + 1) * G],
                       reads=[XN(kc)], writes=["xTd%d_%d" % (g, kc)])
        else:
            bank, bn = gbank()
            for kc in range(KC):
                sq, sn = (sqbuf() if STOP.get("F", 1) else tmp())
                op("act", lambda e, kc=kc, sq=sq: e.activation(out=sq[:, 0:G], in_=xT[:, kc * G:(kc + 1) * G], func=AF.Square), [XN(kc)], [sn])
                op("pe", lambda e, kc=kc, sq=sq: e.matmul(bank[:, 0:G], lhsT=(ones_b[:] if STOP.get("F", 1) else ones_f), rhs=sq[:, 0:G], start=(kc == 0), stop=(kc == KC - 1)),
                   [sn, "ones_b", "consts"], [bn])
            rs, rn = tmp()
            op("act", lambda e: e.activation(out=rs[:, 0:G], in_=bank[:, 0:G], func=AF.Sqrt, bias=eps_ap, scale=1.0), [bn, "consts"], [rn])
            op("dve", lambda e: e.reciprocal(out=rs[:, 0:G], in_=rs[:, 0:G]), [rn], [rn])
            for kc in range(KC):
                op("dve", lambda e, kc=kc: e.scalar_tensor_tensor(out=xT[:, kc * G:(kc + 1) * G], in0=xT[:, kc * G:(kc + 1) * G],
                                                                 scalar=smalls[:, O_FG + kc:O_FG + kc + 1], in1=rs[:, 0:G],
                                                                 op0=ALU.mult, op1=ALU.mult), [XN(kc), rn, "smalls"], [XN(kc)])
            for b in range(4):
                for hf in range(2):
                    bank, bn = gbank()
                    rd = [XN(hf * 4 + cc) for cc in range(4)] + ["consts"]
                    S_.begin("pe", rd, [bn])
                    inst = None
                    for cc in range(4):
                        kc = hf * 4 + cc
                        inst = nc.tensor.transpose(out=bank[:, cc * 128:(cc + 1) * 128], in_=xT[:, kc * G + b * 128:kc * G + (b + 1) * 128], identity=ident_f)
                    S_.end("pe", inst, rd, [bn])
                    tb, tn = tmp()
                    evac(tb[:, 0:512], bank[:, 0:512], [bn], [tn])
                    S_.dma("pool", "x", out=out_d[t0 + b * 128:t0 + (b + 1) * 128, hf * 512:(hf + 1) * 512], in_=tb[:, 0:512],
                           reads=[tn], writes=["outd"])

    per_g = (48 + TG - 1) // TG
    prepare_x(0, 0, 0)
    norm_stats(xTb[0], (lambda kc: "xT0_%d" % kc), rsn, "rsn")
    for l in range(L):
        for g in range(TG):
            cis = list(range(g * per_g, min(48, (g + 1) * per_g))) if l + 1 < L else []
            pre = 0 < len(cis) <= 6
            if pre:
                c1, c2 = cis[:3], cis[3:]
                emit_mod_load(l + 1, c1)

                def hook(l=l, c1=c1, c2=c2):
                    emit_mod_mm(l + 1, c1)
                    emit_mod_load(l + 1, c2)
                layer_group(l, g, hook)
                emit_mod_mm(l + 1, c2)
            else:
                layer_group(l, g)
                if l + 1 < L:
                    emit_mod(l + 1, cis)
            if g == 0 and l + 1 < L:
                emit_convert(l + 1)
        if l + 1 < L:
            emit_der(l + 1)
    if dbg_d is not None:
        S_.dma("sp", "m", out=dbg_d[:, 0:48], in_=modv[:, 0:48], reads=["modv0"], writes=["dbgd"])
        S_.dma("sp", "m", out=dbg_d[:, 48:64], in_=der[:, 0:16], reads=["der0"], writes=["dbgd2"])
    S_.wait_all("pool")
    S_.wait_all("sp")
    es.close()
    return nc


def make_consts():
    c = np.zeros((P, NCON), np.float32)
    c[:, C_ID:C_ID + 128] = np.eye(128, dtype=np.float32)
    c[:, C_ONES:C_ONES + 128] = 1.0 / 1024.0
    si = np.arange(128)[:, None]
    qi = np.arange(128)[None, :]
    c[:, C_TRI:C_TRI + 128] = (si <= qi).astype(np.float32)
    for h in range(4):
        slope = 2.0 ** (-8.0 * (h + 1) / 4)
        for di in range(ND):
            dist = di - 3
            c[:, C_DB + h * ND + di] = slope * (np.arange(128) - 127) - slope * 128.0 * dist
    c[:, C_EPS] = EPS
    sw = np.zeros((P, 2, 2, 4, 128), np.float32)
    for which in range(2):
        for half in range(2):
            for cc in range(4):
                h = 2 * cc + half
                slope = 2.0 ** (-(h + 1))
                if which == 1:
                    delta = qi - si
                    valid = delta >= 0
                else:
                    delta = qi + 128 - si
                    valid = delta < 128
                sw[:, which, half, cc, :] = np.where(valid, -slope * delta, -30000.0)
    return c, sw.reshape(P, 2048)


def pm(v):
    v = np.asarray(v, np.float32)
    return np.ascontiguousarray(v.reshape(-1, P).T)


def make_smalls(b, inp, L):
    NS = L * LS + 16 + 8
    s = np.zeros((P, NS), np.float32)
    for l in range(L):
        o = l * LS
        s[:, o + O_G1:o + O_G1 + 8] = pm(inp["norm1_g"][l])
        s[:, o + O_G2:o + O_G2 + 8] = pm(inp["norm2_g"][l])
        for k in range(3):
            s[:, o + O_CW + k * NJ:o + O_CW + (k + 1) * NJ] = pm(inp["conv_w"][l, k])
        s[:, o + O_CB:o + O_CB + NJ] = pm(inp["conv_b"][l])
        s[:, o + O_BADA:o + O_BADA + 48] = pm(inp["b_ada"][l])
        s[:, o + O_SINK:o + O_SINK + 8] = np.broadcast_to(np.asarray(inp["swa_sinks"][l], np.float32)[None, :], (P, 8))
        s[:, o + O_LAM:o + O_LAM + 256] = np.broadcast_to(np.asarray(inp["diff_lambda"][l], np.float32).reshape(1, 256), (P, 256))
        s[:, o + O_SUBG] = np.asarray(inp["diff_subln_g"][l], np.float32)
    cc = pm(inp["c"][b])
    s[:, L * LS:L * LS + 16] = np.repeat(cc, 2, axis=1)
    s[:, L * LS + 16:L * LS + 24] = pm(inp["final_g"])
    return s


_CACHE = {}


def run(inp, S, L, n_cores):
    key = (S, L)
    if key not in _CACHE:
        _CACHE[key] = build(S, L)
    nc = _CACHE[key]
    consts, swab = make_consts()
    f = lambda a: np.ascontiguousarray(np.asarray(a, np.float32))
    shared = {
        "consts": consts, "swab": swab,
        "w_ada": f(inp["w_ada"][:L]), "w_in": f(inp["w_in"][:L]), "w_bs": f(inp["w_branch_swa"][:L]),
        "w_bd": f(inp["w_branch_diff"][:L]), "w_out": f(inp["w_out"][:L]), "w_up": f(inp["w_up"][:L]),
        "w_dn": f(inp["w_down"][:L]),
    }
    real_slots = [0, 1, 4, 5][:n_cores] if (n_cores == 4 and not STOP.get("nodummy")) else list(range(n_cores))
    n_launch = 8 if real_slots != list(range(n_cores)) else n_cores
    zero_shared = None
    in_maps = []
    for slot in range(n_launch):
        if slot in real_slots:
            b = real_slots.index(slot)
            m = dict(shared)
            m["x"] = f(inp["x"][b, :S])
            m["smalls"] = make_smalls(b, inp, L)
        else:
            if zero_shared is None:
                zero_shared = {k: np.zeros_like(v) for k, v in shared.items()}
                zero_shared["consts"] = consts
                zero_shared["swab"] = swab
            m = dict(zero_shared)
            m["x"] = np.zeros((S, D), np.float32)
            m["smalls"] = np.zeros((P, L * LS + 24), np.float32)
        in_maps.append(m)
    res = run_bass_kernel_spmd(nc, in_maps, core_ids=list(range(n_launch)))
    if STOP.get("dbg"):
        STOP["dbg_out"] = [np.asarray(r["dbg"]) for r in res.results]
    return np.stack([np.asarray(res.results[slot]["out"], np.float32) for slot in real_slots], axis=0)


def kernel(**inputs):
    inp = {k: np.asarray(v) for k, v in inputs.items()}
    B, S, _ = inp["x"].shape
    L = inp["w_in"].shape[0]
    return run(inp, S, L, B)
```

```python
import math
from contextlib import ExitStack
import numpy as np
import concourse.bass as bass
import concourse.mybir as mybir
from concourse.bass_utils import run_bass_kernel_spmd

F32 = mybir.dt.float32
BF16 = mybir.dt.bfloat16
AF = mybir.ActivationFunctionType
ALU = mybir.AluOpType
AX = mybir.AxisListType

P = 128
D = 1024
KC = 8
DFF = 2816
NJ = 22
INC = 4352
G = 512
EPS = 1e-6
ND = 36
NBUF = 6
USZ = 2048
NTMP = 7
NPT = 6

O_G1, O_G2, O_CW, O_CB, O_BADA, O_SINK, O_LAM, O_SUBG = 0, 8, 16, 82, 104, 152, 160, 416
LS = 417
C_ID, C_ONES, C_TRI, C_DB, C_EPS = 0, 128, 256, 384, 384 + 4 * ND
NCON = C_EPS + 1


STOP = {}


def lam_init_of(l):
    return 0.8 - 0.6 * math.exp(-0.3 * l)


class Sched:
    def __init__(self, nc):
        self.nc = nc
        self.eng = {}
        self.semh = {}
        for name, h in [("pe", nc.tensor), ("act", nc.scalar), ("dve", nc.vector),
                        ("pool", nc.gpsimd), ("sp", nc.sync)]:
            sem = nc.alloc_semaphore(name="s_" + name)
            self.eng[name] = dict(h=h, sem=sem, cnt=0, waited={})
            self.semh[name] = sem
        self.res = {}
        self.rings = {}
        self.n_wait = 0

    def add_ring(self, rname, n):
        lst = []
        for i in range(n):
            key = "d_%s_%d" % (rname, i)
            sem = self.nc.alloc_semaphore(name=key)
            self.semh[key] = sem
            lst.append(dict(key=key, sem=sem, val=0))
        self.rings[rname] = dict(lst=lst, rr=0)

    def _wait(self, ename, tok):
        key, val = tok
        e = self.eng[ename]
        if e["waited"].get(key, 0) >= val:
            return
        e["h"].wait_ge(self.semh[key], val)
        e["waited"][key] = val
        self.n_wait += 1

    def begin(self, ename, reads=(), writes=()):
        deps = []
        for r in reads:
            st = self.res.get(r)
            if st:
                deps += st["w"]
        for w in writes:
            st = self.res.get(w)
            if st:
                deps += st["w"] + list(st["r"].values())
        for tok in deps:
            if ename == "pe" and tok[0] == "pe":
                continue
            self._wait(ename, tok)

    def _record(self, tok, reads, writes):
        for r in reads:
            st = self.res.setdefault(r, dict(w=[], r={}))
            st["r"][tok[0]] = tok
        for w in writes:
            self.res[w] = dict(w=[tok], r={})

    def end(self, ename, inst, reads=(), writes=()):
        e = self.eng[ename]
        e["cnt"] += 1
        inst.then_inc(e["sem"], 1)
        tok = (ename, e["cnt"])
        self._record(tok, reads, writes)
        return tok

    def op(self, ename, fn, reads=(), writes=()):
        self.begin(ename, reads, writes)
        inst = fn(self.eng[ename]["h"])
        return self.end(ename, inst, reads, writes)

    def dma(self, q, ring, out, in_, reads=(), writes=()):
        rg = self.rings[ring]
        slot = rg["lst"][rg["rr"] % len(rg["lst"])]
        rg["rr"] += 1
        if slot["val"] > 0:
            self._wait(q, (slot["key"], slot["val"]))
        self.begin(q, reads, writes)
        slot["val"] += 16
        self.eng[q]["h"].dma_start(out=out, in_=in_).then_inc(slot["sem"], 16)
        tok = (slot["key"], slot["val"])
        self._record(tok, reads, writes)
        return tok

    def wait_all(self, ename):
        for st in self.res.values():
            for tok in st["w"] + list(st["r"].values()):
                self._wait(ename, tok)


def build(S, L):
    nc = bass.Bass("TRN2", target_bir_lowering=False)
    TG = S // G
    NBS = S // P
    NS = L * LS + 16 + 8
    O_C = L * LS
    O_FG = L * LS + 16
    dt = nc.dram_tensor
    x_d = dt("x", [S, D], F32, kind="ExternalInput").ap()
    smalls_d = dt("smalls", [P, NS], F32, kind="ExternalInput").ap()
    consts_d = dt("consts", [P, NCON], F32, kind="ExternalInput").ap()
    swab_d = dt("swab", [P, 2048], F32, kind="ExternalInput").ap()
    wada_d = dt("w_ada", [L, D, 6 * D], F32, kind="ExternalInput").ap()
    win_d = dt("w_in", [L, D, INC], F32, kind="ExternalInput").ap()
    wbs_d = dt("w_bs", [L, 512, D], F32, kind="ExternalInput").ap()
    wbd_d = dt("w_bd", [L, 512, D], F32, kind="ExternalInput").ap()
    wout_d = dt("w_out", [L, D, D], F32, kind="ExternalInput").ap()
    wup_d = dt("w_up", [L, D, 2 * DFF], F32, kind="ExternalInput").ap()
    wdn_d = dt("w_dn", [L, DFF, D], F32, kind="ExternalInput").ap()
    out_d = dt("out", [S, D], F32, kind="ExternalOutput").ap()
    dbg_d = dt("dbg", [P, 64], F32, kind="ExternalOutput").ap() if STOP.get("dbg") else None
    xT_d = dt("xT_scr", [D, S], F32, kind="Internal").ap()
    winb = [dt("winb%d" % l, [D, INC], BF16, kind="Internal").ap() for l in range(L)]
    wbsb = [dt("wbsb%d" % l, [512, D], BF16, kind="Internal").ap() for l in range(L)]
    wbdb = [dt("wbdb%d" % l, [512, D], BF16, kind="Internal").ap() for l in range(L)]
    woutb = [dt("woutb%d" % l, [D, D], BF16, kind="Internal").ap() for l in range(L)]
    wupb = [dt("wupb%d" % l, [D, 2 * DFF], BF16, kind="Internal").ap() for l in range(L)]
    wdnb = [dt("wdnb%d" % l, [DFF, D], BF16, kind="Internal").ap() for l in range(L)]

    es = ExitStack()
    sb = lambda name, shape, dtype: es.enter_context(nc.sbuf_tensor("sb_" + name, shape, dtype))
    xTb = [sb("xTbuf%d" % i, [P, KC * G], F32) for i in range(2)]
    hT = sb("hT", [P, KC * G], BF16)
    QM = sb("QM", [P, 8 * G], BF16)
    KdT = sb("KdT", [P, 4 * S], BF16)
    VdE = sb("VdE", [P, NBS * 4 * 130], BF16)
    KaT = sb("KaT", [P, 2 * 640], BF16)
    VaE = sb("VaE", [P, 5 * 132], BF16)
    oa_tok = sb("oa_tok", [P, 512], BF16)
    od_tok = [sb("od_tok%d" % i, [P, 128], BF16) for i in range(4)]
    oaT = sb("oaT", [P, 4 * G], BF16)
    odT = sb("odT", [P, 4 * G], BF16)
    ring = [sb("ring%d" % i, [P, USZ], BF16) for i in range(NBUF)]
    actT = [sb("actT%d" % i, [P, 4 * G], BF16) for i in range(2)]
    tmps = [sb("tmp%d" % i, [P, 520], F32) for i in range(NTMP)]
    PTs = [sb("PT%d" % i, [P, 512], BF16) for i in range(NPT)]
    sigs = [sb("sig%d" % i, [P, 512], BF16) for i in range(2)]
    adabig = sb("adabig", [P, KC * 384], BF16)
    consts = sb("consts", [P, NCON], F32)
    swaB = sb("swaB", [P, 2048], BF16)
    smalls = sb("smalls", [P, NS], F32)
    ident_b = sb("ident_b", [P, 128], BF16)
    ones_b = sb("ones_b", [P, 128], BF16)
    sqb = [sb("sqb%d" % i, [P, 512], BF16) for i in range(2)]
    tri_b = sb("tri_b", [P, 128], BF16)
    cs2 = sb("cs2", [P, 16], BF16)
    modv = sb("modv", [P, L * 48], F32)
    der = sb("der", [P, L * 16], F32)
    misc = sb("misc", [P, L * 16], F32)
    halo = sb("halo", [P, NJ * 2], F32)
    rsn = sb("rsn", [P, G], F32)
    ps = [es.enter_context(nc.psum_tensor("ps%d" % i, [P, 512], F32)) for i in range(8)]

    S_ = Sched(nc)
    S_.add_ring("w", NBUF)
    S_.add_ring("x", 16)
    S_.add_ring("cv", 8)
    S_.add_ring("ada", 4)
    S_.add_ring("m", 4)
    op = S_.op

    ident_f = consts[:, C_ID:C_ID + 128]
    ones_f = consts[:, C_ONES:C_ONES + 128]
    eps_ap = consts[:, C_EPS:C_EPS + 1]

    st = dict(gb=0, tmp=0, pt=0, sig=0, ring=0, od=0, ada=0, ev=0, sq=0, banks=list(range(STOP.get("nb", 8))))

    def gbank():
        pool_ = st["banks"]
        i = pool_[st["gb"] % len(pool_)]
        st["gb"] += 1
        return ps[i], "ps%d" % i

    def tmp():
        i = st["tmp"] % NTMP
        st["tmp"] += 1
        return tmps[i], "tmp%d" % i

    def sqbuf():
        i = st["sq"] % 2
        st["sq"] += 1
        return sqb[i], "sqb%d" % i

    def ptbuf():
        i = st["pt"] % NPT
        st["pt"] += 1
        return PTs[i], "PT%d" % i

    def sigbuf():
        i = st["sig"] % 2
        st["sig"] += 1
        return sigs[i], "sig%d" % i

    def evac(out, in_, reads, writes, eng=None):
        if eng is None:
            bk = [r for r in reads if r.startswith("ps")]
            eng = "act" if (bk and int(bk[0][2:]) % 2 == 0) else "dve"
        if eng == "act":
            return op("act", lambda e: e.activation(out=out, in_=in_, func=AF.Copy), reads, writes)
        return op(eng, lambda e: e.tensor_copy(out=out, in_=in_), reads, writes)

    def mm_group(out, pairs, reads, writes):
        late = [r for r in reads if r.startswith("hT")] if len(pairs) == KC else []
        early = [r for r in reads if r not in late]
        S_.begin("pe", early, writes)
        n = len(pairs)
        inst = None
        for i, (l_, r_) in enumerate(pairs):
            if late:
                S_.begin("pe", ["hT%d" % i], ())
            inst = nc.tensor.matmul(out, lhsT=l_, rhs=r_, start=(i == 0), stop=(i == n - 1))
        return S_.end("pe", inst, reads, writes)

    def load_unit(src, kcn, ncols, reads):
        i = st["ring"] % NBUF
        st["ring"] += 1
        dst = ring[i][:, 0:kcn * ncols].rearrange("p (k n) -> p k n", n=ncols)
        S_.dma("sp", "w", out=dst, in_=src, reads=reads, writes=["ring%d" % i])
        return ring[i], "ring%d" % i

    def wsrc(w, r0, nr, c0, ncols):
        return w[r0:r0 + nr, c0:c0 + ncols].rearrange("(kc p) n -> p kc n", p=P)

    S_.dma("sp", "m", out=consts[:], in_=consts_d[:, :], writes=["consts"])
    S_.dma("sp", "m", out=smalls[:], in_=smalls_d[:, :], writes=["smalls"])
    S_.dma("pool", "m", out=swaB[:], in_=swab_d[:, :], writes=["swaB"])
    op("dve", lambda e: e.tensor_copy(out=ident_b[:], in_=consts[:, C_ID:C_ID + 128]), ["consts"], ["ident_b"])
    op("dve", lambda e: e.tensor_copy(out=tri_b[:], in_=consts[:, C_TRI:C_TRI + 128]), ["consts"], ["tri_b"])
    op("dve", lambda e: e.tensor_copy(out=ones_b[:], in_=consts[:, C_ONES:C_ONES + 128]), ["consts"], ["ones_b"])
    op("pool", lambda e: e.memset(VdE[:], 1.0), [], ["VdE%d" % b for b in range(NBS)])
    op("pool", lambda e: e.memset(VaE[:], 1.0), [], ["VaEprev", "VaEcur"])
    op("act", lambda e: e.activation(out=cs2[:, 0:16], in_=smalls[:, O_C:O_C + 16], func=AF.Silu), ["smalls"], ["cs2"])

    def emit_convert(l):
        for (dst, src, rn) in [(winb[l], win_d[l], "winb"), (wbsb[l], wbs_d[l], "wbsb"), (wbdb[l], wbd_d[l], "wbdb"),
                               (woutb[l], wout_d[l], "woutb"), (wupb[l], wup_d[l], "wupb"), (wdnb[l], wdn_d[l], "wdnb")]:
            S_.dma("pool", "cv", out=dst[:, :], in_=src, writes=["%s%d" % (rn, l)])

    def emit_mod_load(l, cis):
        cis = list(cis)
        n = len(cis)
        if n == 0:
            return
        S_.dma("pool", "ada", out=adabig[:, 0:KC * n * 128].rearrange("p (k n) -> p k n", n=n * 128),
               in_=wsrc(wada_d[l], 0, D, cis[0] * 128, n * 128), writes=["adabig"])

    def emit_mod_mm(l, cis):
        cis = list(cis)
        n = len(cis)
        for ii, ci in enumerate(cis):
            bank, bn = gbank()
            mm_group(bank[:, 0:2], [(adabig[:, kc * n * 128 + ii * 128:kc * n * 128 + (ii + 1) * 128], cs2[:, 2 * kc:2 * kc + 2]) for kc in range(KC)],
                     ["adabig", "cs2"], [bn])
            op("dve", lambda e, ci=ci, bank=bank: e.tensor_tensor(
                out=modv[:, l * 48 + ci:l * 48 + ci + 1], in0=bank[:, 0:1],
                in1=smalls[:, l * LS + O_BADA + ci:l * LS + O_BADA + ci + 1], op=ALU.add),
               [bn, "smalls"], ["modv%d" % l])

    def emit_mod(l, cis):
        cis = list(cis)
        for k in range(0, len(cis), 3):
            emit_mod_load(l, cis[k:k + 3])
            emit_mod_mm(l, cis[k:k + 3])

    def emit_der(l):
        for (o0, m0, g0) in [(0, 8, O_G1), (8, 32, O_G2)]:
            op("dve", lambda e, o0=o0, m0=m0, g0=g0: e.scalar_tensor_tensor(
                out=der[:, l * 16 + o0:l * 16 + o0 + 8], in0=modv[:, l * 48 + m0:l * 48 + m0 + 8], scalar=1.0,
                in1=smalls[:, l * LS + g0:l * LS + g0 + 8], op0=ALU.add, op1=ALU.mult),
               ["modv%d" % l, "smalls"], ["der%d" % l])

    def emit_misc(l):
        b = l * 16
        li = lam_init_of(l)
        lo = l * LS + O_LAM
        op("act", lambda e: e.activation(out=misc[:, b:b + 8], in_=smalls[:, l * LS + O_SINK:l * LS + O_SINK + 8], func=AF.Exp),
           ["smalls"], ["misc"])
        t, tn = tmp()
        for k in range(2):
            op("dve", lambda e, k=k: e.tensor_tensor(out=t[:, k * 64:(k + 1) * 64], in0=smalls[:, lo + k * 128:lo + k * 128 + 64],
                                                     in1=smalls[:, lo + k * 128 + 64:lo + k * 128 + 128], op=ALU.mult), ["smalls"], [tn])
            op("dve", lambda e, k=k: e.tensor_reduce(out=misc[:, b + 10 + k:b + 11 + k], in_=t[:, k * 64:(k + 1) * 64], axis=AX.X, op=ALU.add),
               [tn], ["misc"])
        op("act", lambda e: e.activation(out=misc[:, b + 12:b + 14], in_=misc[:, b + 10:b + 12], func=AF.Exp), ["misc"], ["misc"])
        op("dve", lambda e: e.tensor_tensor(out=misc[:, b + 8:b + 9], in0=misc[:, b + 13:b + 14], in1=misc[:, b + 12:b + 13], op=ALU.subtract),
           ["misc"], ["misc"])
        op("dve", lambda e: e.tensor_scalar(out=misc[:, b + 8:b + 9], in0=misc[:, b + 8:b + 9], scalar1=-li, scalar2=None, op0=ALU.add),
           ["misc"], ["misc"])
        op("dve", lambda e: e.tensor_scalar(out=misc[:, b + 9:b + 10], in0=smalls[:, l * LS + O_SUBG:l * LS + O_SUBG + 1],
                                            scalar1=(1.0 - li), scalar2=None, op0=ALU.mult), ["smalls", "misc"], ["misc"])

    emit_convert(0)
    for l in range(L):
        emit_misc(l)
    emit_mod(0, range(48))
    emit_der(0)

    def norm_stats(xT, XN, rs, rn):
        bank, bn = gbank()
        for kc in range(KC):
            sq, sn = (sqbuf() if STOP.get("F", 1) else tmp())
            op("act", lambda e, kc=kc, sq=sq: e.activation(out=sq[:, 0:G], in_=xT[:, kc * G:(kc + 1) * G], func=AF.Square),
               [XN(kc)], [sn])
            op("pe", lambda e, kc=kc, sq=sq: e.matmul(bank[:, 0:G], lhsT=(ones_b[:] if STOP.get("F", 1) else ones_f), rhs=sq[:, 0:G], start=(kc == 0), stop=(kc == KC - 1)),
               [sn, "ones_b", "consts"], [bn])
        op("act", lambda e: e.activation(out=rs[:, 0:G], in_=bank[:, 0:G], func=AF.Sqrt, bias=eps_ap, scale=1.0), [bn, "consts"], [rn])
        op("dve", lambda e: e.reciprocal(out=rs[:, 0:G], in_=rs[:, 0:G]), [rn], [rn])

    def norm_to_hT(l, which, xT, XN, pre_rs=None):
        if pre_rs is None:
            rs, rn = tmp()
            norm_stats(xT, XN, rs, rn)
        else:
            rs, rn = pre_rs
        so = l * 16 + (0 if which == 1 else 8)
        sho = l * 48 + (0 if which == 1 else 24)
        for kc in range(KC):
            t, tn = tmp()
            if tn == rn:
                t, tn = tmp()
            op("dve", lambda e, kc=kc, t=t: e.scalar_tensor_tensor(out=t[:, 0:G], in0=xT[:, kc * G:(kc + 1) * G],
                                                                  scalar=der[:, so + kc:so + kc + 1], in1=rs[:, 0:G],
                                                                  op0=ALU.mult, op1=ALU.mult),
               [XN(kc), rn, "der%d" % l], [tn])
            op("act", lambda e, kc=kc, t=t: e.activation(out=hT[:, kc * G:(kc + 1) * G], in_=t[:, 0:G], func=AF.Identity,
                                                         bias=modv[:, sho + kc:sho + kc + 1], scale=1.0),
               [tn, "modv%d" % l], ["hT%d" % kc])

    HT_ALL = ["hT%d" % kc for kc in range(KC)]

    def prepare_x(l, g, p):
        xT = xTb[p]
        XN = lambda kc: "xT%d_%d" % (p, kc)
        t0 = g * G
        if l == 0:
            for b in range(4):
                for hf in range(2):
                    tb, tn = tmp()
                    S_.dma("pool", "x", out=tb[:, 0:512], in_=x_d[t0 + b * 128:t0 + (b + 1) * 128, hf * 512:(hf + 1) * 512], writes=[tn])
                    bank, bn = gbank()
                    S_.begin("pe", [tn, "consts"], [bn])
                    inst = None
                    for cc in range(4):
                        inst = nc.tensor.transpose(out=bank[:, cc * 128:(cc + 1) * 128], in_=tb[:, cc * 128:(cc + 1) * 128], identity=ident_f)
                    S_.end("pe", inst, [tn, "consts"], [bn])
                    outv = xT[:, hf * 4 * G:(hf + 1) * 4 * G].rearrange("p (c t) -> p c t", t=G)[:, :, b * 128:(b + 1) * 128]
                    evac(outv, bank[:, 0:512].rearrange("p (c t) -> p c t", t=128), [bn], [XN(hf * 4 + cc) for cc in range(4)])
        else:
            for kc in range(KC):
                S_.dma("pool", "x", out=xT[:, kc * G:(kc + 1) * G], in_=xT_d[kc * P:(kc + 1) * P, t0:t0 + G],
                       reads=["xTd%d_%d" % (g, kc)], writes=[XN(kc)])

    def layer_group(l, g, mid_hook=None):
        t0 = g * G
        last_layer = (l == L - 1)
        p = (l * TG + g) % 2
        xT = xTb[p]
        XN = lambda kc: "xT%d_%d" % (p, kc)
        if STOP.get('s', 99) == 1:
            return
        norm_to_hT(l, 1, xT, XN, pre_rs=(rsn, "rsn"))
        if STOP.get('s', 99) == 2:
            return
        wl = "winb%d" % l
        for uh in range(2):
            u, un = load_unit(wsrc(winb[l], 0, D, 1792 + uh * 256, 256), KC, 256, [wl])
            for bp in range(2):
                bank, bn = gbank()
                for bb in range(2):
                    b = bp * 2 + bb
                    mm_group(bank[:, bb * 256:(bb + 1) * 256],
                             [(hT[:, kc * G + b * 128:kc * G + (b + 1) * 128], u[:, kc * 256:(kc + 1) * 256]) for kc in range(KC)],
                             HT_ALL + [un], [bn])
                if STOP.get('s', 99) == 211:
                    return
                for bb in range(2):
                    b = bp * 2 + bb
                    blk = 4 * g + b
                    o0 = (blk * 4 + 2 * uh) * 130
                    for hh in range(2):
                        evac(VdE[:, o0 + hh * 130:o0 + hh * 130 + 128], bank[:, bb * 256 + hh * 128:bb * 256 + (hh + 1) * 128], [bn], ["VdE%d" % blk])
        if STOP.get('s', 99) == 21:
            return
        u, un = load_unit(wsrc(winb[l], 0, D, 512, 256), KC, 256, [wl])
        i = st["ring"] % NBUF
        st["ring"] += 1
        usw, uswn = ring[i], "ring%d" % i
        uswv = usw[:, 0:1024].rearrange("p (k n) -> p k n", n=128)
        S_.dma("sp", "w", out=uswv[:, :, 0:64], in_=wsrc(winb[l], 0, D, 576, 64), reads=[wl], writes=[uswn])
        S_.dma("sp", "w", out=uswv[:, :, 64:128], in_=wsrc(winb[l], 0, D, 512, 64), reads=[wl, uswn], writes=[uswn])
        for r in range(2):
            bank, bn = gbank()
            if r == 0:
                prs = [(u[:, kc * 256:kc * 256 + 128], hT[:, kc * G:(kc + 1) * G]) for kc in range(KC)]
            else:
                prs = [(usw[:, kc * 128:(kc + 1) * 128], hT[:, kc * G:(kc + 1) * G]) for kc in range(KC)]
            mm_group(bank[:, 0:G], prs, HT_ALL + [un, uswn], [bn])
            evac(KaT[:, r * 640 + 128:r * 640 + 640], bank[:, 0:G], [bn], ["KaTcur"])
        if STOP.get('s', 99) == 22:
            return
        bank, bn = gbank()
        for b in range(4):
            mm_group(bank[:, b * 128:(b + 1) * 128],
                     [(hT[:, kc * G + b * 128:kc * G + (b + 1) * 128], u[:, kc * 256 + 128:kc * 256 + 256]) for kc in range(KC)],
                     HT_ALL + [un], [bn])
        for b in range(4):
            for k in range(2):
                evac(VaE[:, (b + 1) * 132 + k * 66:(b + 1) * 132 + k * 66 + 64], bank[:, b * 128 + k * 64:b * 128 + (k + 1) * 64], [bn], ["VaEcur"])
        if STOP.get('s', 99) == 23:
            return
        for (c_base, kind) in [(1280, "kd"), (768, "qd"), (0, "qa")]:
            for uh in range(2):
                u, un = load_unit(wsrc(winb[l], 0, D, c_base + uh * 256, 256), KC, 256, [wl])
                for cc in range(2):
                    ch = uh * 2 + cc
                    bank, bn = gbank()
                    mm_group(bank[:, 0:G], [(u[:, kc * 256 + cc * 128:kc * 256 + (cc + 1) * 128], hT[:, kc * G:(kc + 1) * G]) for kc in range(KC)],
                             HT_ALL + [un], [bn])
                    if kind == "kd":
                        evac(KdT[:, ch * S + t0:ch * S + t0 + G], bank[:, 0:G], [bn], ["KdT%d_%d" % (ch, g)])
                    elif kind == "qd":
                        evac(QM[:, ch * G:(ch + 1) * G], bank[:, 0:G], [bn], ["QM%d" % ch])
                    else:
                        evac(QM[:, (4 + ch) * G:(5 + ch) * G], bank[:, 0:G], [bn], ["QM%d" % (4 + ch)])
        if STOP.get('s', 99) == 3:
            return
        g2 = (g + 1) % TG
        l2 = l if g + 1 < TG else l + 1
        if l2 < L:
            prepare_x(l2, g2, 1 - p)
        st["banks"] = [7]
        QA = ["QM%d" % c for c in range(4, 8)]
        for b in range(4):
            blk = 4 * g + b
            pts = {}
            for which in (0, 1):
                if which == 0 and blk == 0:
                    continue
                if which == 1:
                    kc0, kres = 128 + b * 128, ["KaTcur"]
                elif b == 0:
                    kc0, kres = 0, ["KaTprev"]
                else:
                    kc0, kres = 128 + (b - 1) * 128, ["KaTcur"]
                for half in range(2):
                    bi = half + 2 * which
                    bank, bn = ps[bi], "ps%d" % bi
                    S_.begin("pe", kres + QA, [bn])
                    inst = None
                    for c in range(4):
                        h = 2 * c + half
                        kv = h // 4
                        inst = nc.tensor.matmul(bank[:, c * 128:(c + 1) * 128],
                                                lhsT=KaT[half * 64:(half + 1) * 64, (0 if kv == half else 1) * 640 + kc0:(0 if kv == half else 1) * 640 + kc0 + 128],
                                                rhs=QM[half * 64:(half + 1) * 64, (4 + c) * G + b * 128:(4 + c) * G + (b + 1) * 128],
                                                start=True, stop=True)
                    S_.end("pe", inst, kres + QA, [bn])
                    sc, scn = tmp()
                    bo = (which * 2 + half) * 512
                    op("dve", lambda e, sc=sc, bank=bank, bo=bo: e.scalar_tensor_tensor(
                        out=sc[:, 0:512], in0=bank[:, 0:512], scalar=0.125, in1=swaB[:, bo:bo + 512], op0=ALU.mult, op1=ALU.add),
                       [bn, "swaB"], [scn])
                    pt, ptn = ptbuf()
                    op("act", lambda e, sc=sc, pt=pt: e.activation(out=pt[:, 0:512], in_=sc[:, 0:512], func=AF.Exp), [scn], [ptn])
                    pts[(which, half)] = (pt, ptn)
            if STOP.get('s', 99) == 31:
                return
            whichs = [w_ for w_ in (0, 1) if (w_, 0) in pts]
            rd = [pts[k][1] for k in pts] + ["VaEprev", "VaEcur"]
            S_.begin("pe", rd, ["ps4", "ps5"])
            inst = None
            for h in range(8):
                half, c, kv = h % 2, h // 2, h // 4
                reg = ps[4 + h // 4][:, (h % 4) * 65:(h % 4) * 65 + 65]
                for i, w_ in enumerate(whichs):
                    vb = b + w_
                    inst = nc.tensor.matmul(reg, lhsT=pts[(w_, half)][0][:, c * 128:(c + 1) * 128],
                                            rhs=VaE[:, vb * 132 + kv * 66:vb * 132 + kv * 66 + 65],
                                            start=(i == 0), stop=(i == len(whichs) - 1))
            S_.end("pe", inst, rd, ["ps4", "ps5"])
            if STOP.get('s', 99) == 32:
                return
            sm, smn = tmp()
            S_.begin("dve", ["ps4", "ps5", "misc"], [smn])
            inst = None
            for h in range(8):
                inst = nc.vector.tensor_tensor(out=sm[:, h:h + 1], in0=ps[4 + h // 4][:, (h % 4) * 65 + 64:(h % 4) * 65 + 65],
                                               in1=misc[:, l * 16 + h:l * 16 + h + 1], op=ALU.add)
            S_.end("dve", inst, ["ps4", "ps5", "misc"], [smn])
            op("dve", lambda e, sm=sm: e.reciprocal(out=sm[:, 8:16], in_=sm[:, 0:8]), [smn], [smn])
            S_.begin("dve", [smn, "ps4", "ps5"], ["oa_tok"])
            inst = None
            for h in range(8):
                inst = nc.vector.tensor_scalar(out=oa_tok[:, h * 64:(h + 1) * 64], in0=ps[4 + h // 4][:, (h % 4) * 65:(h % 4) * 65 + 64],
                                               scalar1=sm[:, 8 + h:9 + h], scalar2=None, op0=ALU.mult)
            S_.end("dve", inst, [smn, "ps4", "ps5"], ["oa_tok"])
            if STOP.get('s', 99) == 33:
                return
            tbank, tbn = ps[6], "ps6"
            S_.begin("pe", ["oa_tok", "ident_b"], [tbn])
            inst = None
            for c in range(4):
                inst = nc.tensor.matmul(tbank[:, c * 128:(c + 1) * 128], lhsT=oa_tok[:, c * 128:(c + 1) * 128], rhs=ident_b[:], start=True, stop=True)
            S_.end("pe", inst, ["oa_tok", "ident_b"], [tbn])
            if STOP.get('s', 99) == 335:
                return
            for c in range(4):
                if STOP.get('v') == 1:
                    evac(PTs[c][:, 0:128], tbank[:, c * 128:(c + 1) * 128], [tbn], ["PT%d" % c])
                elif STOP.get('v') == 2:
                    evac(oaT[:, c * G + b * 128:c * G + (b + 1) * 128], tbank[:, c * 128:(c + 1) * 128], [tbn], ["oaT%d" % c], eng="dve")
                elif STOP.get('v') == 3:
                    evac(oaT[:, c * G + b * 128:c * G + (b + 1) * 128], tbank[:, c * 128:(c + 1) * 128], [tbn], ["oaT%d" % c], eng="act")
                else:
                    evac(oaT[:, c * G + b * 128:c * G + (b + 1) * 128], tbank[:, c * 128:(c + 1) * 128], [tbn], ["oaT%d" % c], eng="dve")
            if STOP.get('s', 99) == 34:
                return
        if g < TG - 1:
            for r in range(2):
                op("pool", lambda e, r=r: e.tensor_copy(out=KaT[:, r * 640:r * 640 + 128], in_=KaT[:, r * 640 + 512:r * 640 + 640]),
                   ["KaTcur"], ["KaTprev"])
            op("pool", lambda e: e.tensor_copy(out=VaE[:, 0:132], in_=VaE[:, 528:660]), ["VaEcur"], ["VaEprev"])
        if STOP.get('s', 99) == 4:
            return
        nkb = 4 * g + 4
        for h in range(4):
            W = 1 if h == 0 else 4
            slope_h = 2.0 ** (-2.0 * (h + 1))
            dskip = int(math.ceil((200.0 / slope_h + 127.0) / 128.0))
            kb0 = max(0, 4 * g - dskip + 1)

            def region(m, j):
                r = m * 4 + j
                return ps[4 + r // 3][:, (r % 3) * 130:(r % 3) * 130 + 129], "ps%d" % (4 + r // 3)

            def scores(kb):
                j0 = max(0, kb - 4 * g)
                c0 = j0 * 128
                n = 512 - c0
                res = {}
                for m in range(2):
                    bi = m + 2 * (kb % 2)
                    bank, bn = ps[bi], "ps%d" % bi
                    op("pe", lambda e, m=m, bank=bank: e.matmul(
                        bank[:, 0:n], lhsT=KdT[m * 64:(m + 1) * 64, h * S + kb * 128:h * S + (kb + 1) * 128],
                        rhs=QM[m * 64:(m + 1) * 64, h * G + c0:(h + 1) * G], start=True, stop=True),
                       ["KdT%d_%d" % (h, kb // 4), "QM%d" % h], [bn])
                    pt, ptn = ptbuf()
                    S_.begin("act", [bn, "consts"], [ptn])
                    inst = None
                    if W == 4:
                        di = C_DB + h * ND + (4 * g - kb + 3)
                        inst = nc.scalar.activation(out=pt[:, 0:n], in_=bank[:, 0:n], func=AF.Exp, bias=consts[:, di:di + 1], scale=0.125)
                    else:
                        for j in range(j0, 4):
                            di = C_DB + h * ND + (4 * g + j - kb + 3)
                            a0 = j * 128 - c0
                            inst = nc.scalar.activation(out=pt[:, a0:a0 + 128], in_=bank[:, a0:a0 + 128], func=AF.Exp,
                                                        bias=consts[:, di:di + 1], scale=0.125)
                    S_.end("act", inst, [bn, "consts"], [ptn])
                    if kb >= 4 * g:
                        op("pool", lambda e, pt=pt: e.tensor_tensor(out=pt[:, 0:128], in0=pt[:, 0:128], in1=tri_b[:], op=ALU.mult),
                           [ptn, "tri_b"], [ptn])
                    res[m] = (pt, ptn, c0, j0)
                return res

            def pv(kb, sc):
                for m in range(2):
                    pt, ptn, c0, j0 = sc[m]
                    wr = sorted(set(region(m, j)[1] for j in range(j0, 4)))
                    S_.begin("pe", [ptn, "VdE%d" % kb], wr)
                    inst = None
                    for j in range(j0, 4):
                        reg, _ = region(m, j)
                        a0 = j * 128 - c0
                        vo = (kb * 4 + h) * 130
                        inst = nc.tensor.matmul(reg, lhsT=pt[:, a0:a0 + 128], rhs=VdE[:, vo:vo + 129],
                                                start=(kb == kb0 and (m, j) in ((0, 0), (0, 3), (1, 2))), stop=(kb == 4 * g + j))
                    S_.end("pe", inst, [ptn, "VdE%d" % kb], wr)

            prev = scores(kb0)
            for kb in range(kb0 + 1, nkb):
                cur = scores(kb)
                pv(kb - 1, prev)
                prev = cur
            pv(nkb - 1, prev)
            mo = l * 16
            R0 = [region(0, j) for j in range(4)]
            R1 = [region(1, j) for j in range(4)]
            SM = [tmp() for j in range(4)]
            J4 = range(4)
            for j in J4:
                sm, smn = SM[j]
                op("dve", lambda e, sm=sm, r0=R0[j][0]: e.reciprocal(out=sm[:, 0:1], in_=r0[:, 128:129]), [R0[j][1]], [smn])
            for j in J4:
                sm, smn = SM[j]
                op("dve", lambda e, sm=sm, r1=R1[j][0]: e.reciprocal(out=sm[:, 1:2], in_=r1[:, 128:129]), [R1[j][1], smn], [smn])
            for j in J4:
                sm, smn = SM[j]
                op("dve", lambda e, sm=sm: e.tensor_scalar(out=sm[:, 2:3], in0=sm[:, 1:2], scalar1=misc[:, mo + 8:mo + 9], scalar2=None, op0=ALU.mult),
                   [smn, "misc"], [smn])
            for j in J4:
                sm, smn = SM[j]
                op("dve", lambda e, sm=sm, r0=R0[j][0]: e.tensor_scalar(out=sm[:, 128:256], in0=r0[:, 0:128], scalar1=sm[:, 0:1], scalar2=None, op0=ALU.mult),
                   [R0[j][1], smn], [smn])
            for j in J4:
                sm, smn = SM[j]
                op("dve", lambda e, sm=sm, r1=R1[j][0]: e.scalar_tensor_tensor(out=sm[:, 128:256], in0=r1[:, 0:128], scalar=sm[:, 2:3], in1=sm[:, 128:256],
                                                                            op0=ALU.mult, op1=ALU.add), [R1[j][1], smn], [smn])
            for j in J4:
                sm, smn = SM[j]
                op("act", lambda e, sm=sm: e.activation(out=sm[:, 256:384], in_=sm[:, 128:256], func=AF.Square, accum_out=sm[:, 3:4]), [smn], [smn])
            for j in J4:
                sm, smn = SM[j]
                op("act", lambda e, sm=sm: e.activation(out=sm[:, 4:5], in_=sm[:, 3:4], func=AF.Sqrt, bias=eps_ap, scale=1.0 / 128.0),
                   [smn, "consts"], [smn])
            for j in J4:
                sm, smn = SM[j]
                op("dve", lambda e, sm=sm: e.reciprocal(out=sm[:, 5:6], in_=sm[:, 4:5]), [smn], [smn])
            for j in J4:
                sm, smn = SM[j]
                op("dve", lambda e, sm=sm, j=j: e.tensor_scalar(out=od_tok[j][:], in0=sm[:, 128:256], scalar1=sm[:, 5:6], scalar2=None, op0=ALU.mult),
                   [smn], ["od_tok%d" % j])
            tbank, tbn = gbank()
            rdl = ["od_tok%d" % j for j in J4] + ["ident_b"]
            S_.begin("pe", rdl, [tbn])
            inst = None
            for j in J4:
                inst = nc.tensor.matmul(tbank[:, j * 128:(j + 1) * 128], lhsT=od_tok[j][:], rhs=ident_b[:], start=True, stop=True)
            S_.end("pe", inst, rdl, [tbn])
            op("act", lambda e, tbank=tbank: e.activation(out=odT[:, h * G:(h + 1) * G], in_=tbank[:, 0:G], func=AF.Copy,
                                                        scale=misc[:, mo + 9:mo + 10]), [tbn, "misc"], ["odT%d" % h])
        if STOP.get('s', 99) == 5:
            return
        st["banks"] = list(range(STOP.get("nb", 8)))
        if mid_hook is not None:
            mid_hook()
        if l2 < L:
            norm_stats(xTb[1 - p], (lambda kc: "xT%d_%d" % (1 - p, kc)), rsn, "rsn")
        OA = ["oaT%d" % c for c in range(4)]
        OD = ["odT%d" % c for c in range(4)]
        for op2 in range(4):
            i = st["ring"] % NBUF
            st["ring"] += 1
            ub, ubn = ring[i], "ring%d" % i
            S_.dma("sp", "w", out=ub[:, 0:1024].rearrange("p (k n) -> p k n", n=256), in_=wsrc(wbsb[l], 0, 512, op2 * 256, 256),
                   reads=["wbsb%d" % l], writes=[ubn])
            S_.dma("sp", "w", out=ub[:, 1024:2048].rearrange("p (k n) -> p k n", n=256), in_=wsrc(wbdb[l], 0, 512, op2 * 256, 256),
                   reads=["wbdb%d" % l, ubn], writes=[ubn])
            uga, ugan = load_unit(wsrc(winb[l], 0, D, 2304 + op2 * 256, 256), KC, 256, [wl])
            ugd, ugdn = load_unit(wsrc(winb[l], 0, D, 3328 + op2 * 256, 256), KC, 256, [wl])
            for cc in range(2):
                oc = op2 * 2 + cc
                cs_ = slice(cc * 128, (cc + 1) * 128)
                bga, bgan = gbank()
                mm_group(bga[:, 0:G], [(uga[:, kc * 256 + cc * 128:kc * 256 + (cc + 1) * 128], hT[:, kc * G:(kc + 1) * G]) for kc in range(KC)],
                         HT_ALL + [ugan], [bgan])
                sga, sgan = sigbuf()
                op("act", lambda e, sga=sga, bga=bga: e.activation(out=sga[:, 0:G], in_=bga[:, 0:G], func=AF.Sigmoid), [bgan], [sgan])
                ba, ban = gbank()
                mm_group(ba[:, 0:G], [(ub[:, kc * 256 + cc * 128:kc * 256 + (cc + 1) * 128], oaT[:, kc * G:(kc + 1) * G]) for kc in range(4)],
                         OA + [ubn], [ban])
                t1, t1n = tmp()
                op("dve", lambda e, t1=t1, ba=ba, sga=sga: e.tensor_tensor(out=t1[:, 0:G], in0=ba[:, 0:G], in1=sga[:, 0:G], op=ALU.mult),
                   [ban, sgan], [t1n])
                bgd, bgdn = gbank()
                mm_group(bgd[:, 0:G], [(ugd[:, kc * 256 + cc * 128:kc * 256 + (cc + 1) * 128], hT[:, kc * G:(kc + 1) * G]) for kc in range(KC)],
                         HT_ALL + [ugdn], [bgdn])
                sgd, sgdn = sigbuf()
                op("act", lambda e, sgd=sgd, bgd=bgd: e.activation(out=sgd[:, 0:G], in_=bgd[:, 0:G], func=AF.Sigmoid), [bgdn], [sgdn])
                bd, bdn = gbank()
                mm_group(bd[:, 0:G], [(ub[:, 1024 + kc * 256 + cc * 128:1024 + kc * 256 + (cc + 1) * 128], odT[:, kc * G:(kc + 1) * G]) for kc in range(4)],
                         OD + [ubn], [bdn])
                t2, t2n = tmp()
                op("dve", lambda e, t2=t2, bd=bd, sgd=sgd: e.tensor_tensor(out=t2[:, 0:G], in0=bd[:, 0:G], in1=sgd[:, 0:G], op=ALU.mult),
                   [bdn, sgdn], [t2n])
                op("pool", lambda e, oc=oc, t1=t1, t2=t2: e.tensor_tensor(out=QM[:, oc * G:(oc + 1) * G], in0=t1[:, 0:G], in1=t2[:, 0:G], op=ALU.add),
                   [t1n, t2n], ["QM%d" % oc])
        if STOP.get('s', 99) == 6:
            return
        MG = ["QM%d" % c for c in range(8)]
        for op2 in range(4):
            u, un = load_unit(wsrc(woutb[l], 0, D, op2 * 256, 256), KC, 256, ["woutb%d" % l])
            for cc in range(2):
                oc = op2 * 2 + cc
                bank, bn = gbank()
                mm_group(bank[:, 0:G], [(u[:, kc * 256 + cc * 128:kc * 256 + (cc + 1) * 128], QM[:, kc * G:(kc + 1) * G]) for kc in range(KC)],
                         MG + [un], [bn])
                go = l * 48 + 16 + oc
                op("dve", lambda e, oc=oc, bank=bank, go=go: e.scalar_tensor_tensor(
                    out=xT[:, oc * G:(oc + 1) * G], in0=bank[:, 0:G], scalar=modv[:, go:go + 1], in1=xT[:, oc * G:(oc + 1) * G],
                    op0=ALU.mult, op1=ALU.add), [bn, "modv%d" % l, XN(oc)], [XN(oc)])
        if STOP.get('s', 99) == 7:
            return
        norm_to_hT(l, 2, xT, XN)
        if STOP.get('s', 99) == 8:
            return
        if g == 0:
            op("pool", lambda e: e.memset(halo[:], 0.0), [], ["halo"])
        cwo = l * LS + O_CW
        cbo = l * LS + O_CB
        def ffn_up(q):
            npair = 2 if q < 5 else 1
            ai = q % 2
            at, atn = actT[ai], "actT%d" % ai
            for pr in range(npair):
                j0 = q * 4 + pr * 2
                ua_u, ua_n = load_unit(wsrc(wupb[l], 0, D, j0 * 128, 256), KC, 256, ["wupb%d" % l])
                ug_u, ug_n = load_unit(wsrc(wupb[l], 0, D, DFF + j0 * 128, 256), KC, 256, ["wupb%d" % l])
                for cc in range(2):
                    j = j0 + cc
                    jj = pr * 2 + cc
                    ba, ban = gbank()
                    mm_group(ba[:, 0:G], [(ua_u[:, kc * 256 + cc * 128:kc * 256 + (cc + 1) * 128], hT[:, kc * G:(kc + 1) * G]) for kc in range(KC)],
                             HT_ALL + [ua_n], [ban])
                    ua, uan = tmp()
                    op("pool", lambda e, ua=ua, j=j: e.tensor_copy(out=ua[:, 0:2], in_=halo[:, 2 * j:2 * j + 2]), ["halo"], [uan])
                    op("act", lambda e, ua=ua, ba=ba: e.activation(out=ua[:, 2:2 + G], in_=ba[:, 0:G], func=AF.Copy), [ban, uan], [uan])
                    op("pool", lambda e, ua=ua, j=j: e.tensor_copy(out=halo[:, 2 * j:2 * j + 2], in_=ua[:, G:G + 2]), [uan], ["halo"])
                    cv, cvn = tmp()
                    op("dve", lambda e, ua=ua, cv=cv, j=j: e.tensor_scalar(
                        out=cv[:, 0:G], in0=ua[:, 2:2 + G], scalar1=smalls[:, cwo + 2 * NJ + j:cwo + 2 * NJ + j + 1],
                        scalar2=smalls[:, cbo + j:cbo + j + 1], op0=ALU.mult, op1=ALU.add), [uan, "smalls"], [cvn])
                    op("dve", lambda e, ua=ua, cv=cv, j=j: e.scalar_tensor_tensor(
                        out=cv[:, 0:G], in0=ua[:, 1:1 + G], scalar=smalls[:, cwo + NJ + j:cwo + NJ + j + 1], in1=cv[:, 0:G],
                        op0=ALU.mult, op1=ALU.add), [uan, "smalls", cvn], [cvn])
                    op("dve", lambda e, ua=ua, cv=cv, j=j: e.scalar_tensor_tensor(
                        out=cv[:, 0:G], in0=ua[:, 0:G], scalar=smalls[:, cwo + j:cwo + j + 1], in1=cv[:, 0:G],
                        op0=ALU.mult, op1=ALU.add), [uan, "smalls", cvn], [cvn])
                    op("act", lambda e, cv=cv: e.activation(out=cv[:, 0:G], in_=cv[:, 0:G], func=AF.Gelu), [cvn], [cvn])
                    bg, bgn = gbank()
                    mm_group(bg[:, 0:G], [(ug_u[:, kc * 256 + cc * 128:kc * 256 + (cc + 1) * 128], hT[:, kc * G:(kc + 1) * G]) for kc in range(KC)],
                             HT_ALL + [ug_n], [bgn])
                    op("dve", lambda e, cv=cv, bg=bg, jj=jj, at=at: e.tensor_tensor(out=at[:, jj * G:(jj + 1) * G], in0=bg[:, 0:G], in1=cv[:, 0:G], op=ALU.mult),
                       [bgn, cvn], [atn])

        def ffn_down(q):
            npair = 2 if q < 5 else 1
            ai = q % 2
            at, atn = actT[ai], "actT%d" % ai
            nkc = npair * 2
            dus = []
            for pr in range(npair):
                j0 = q * 4 + pr * 2
                dus.append(load_unit(wsrc(wdnb[l], j0 * 128, 256, 0, D), 2, D, ["wdnb%d" % l]))
            for oc in range(KC):
                bank, bn = gbank()
                pairs = []
                for jj in range(nkc):
                    du = dus[jj // 2][0]
                    pairs.append((du[:, (jj % 2) * D + oc * 128:(jj % 2) * D + (oc + 1) * 128], at[:, jj * G:(jj + 1) * G]))
                mm_group(bank[:, 0:G], pairs, [atn] + [d_[1] for d_ in dus], [bn])
                go = l * 48 + 40 + oc
                op("dve", lambda e, oc=oc, bank=bank, go=go: e.scalar_tensor_tensor(
                    out=xT[:, oc * G:(oc + 1) * G], in0=bank[:, 0:G], scalar=modv[:, go:go + 1], in1=xT[:, oc * G:(oc + 1) * G],
                    op0=ALU.mult, op1=ALU.add), [bn, "modv%d" % l, XN(oc)], [XN(oc)])

        ffn_up(0)
        for q in range(6):
            if q + 1 < 6:
                ffn_up(q + 1)
            ffn_down(q)
        if STOP.get('s', 99) == 9:
            return
        if not last_layer:
            for kc in range(KC):
                S_.dma("pool", "x", out=xT_d[kc * P:(kc + 1) * P, t0:t0 + G], in_=xT[:, kc * G:(kc + 1) * G],
                       reads=[XN(kc)], writes=["xTd%d_%d" % (g, kc)])
        else:
            bank, bn = gbank()
            for kc in range(KC):
                sq, sn = (sqbuf() if STOP.get("F", 1) else tmp())
                op("act", lambda e, kc=kc, sq=sq: e.activation(out=sq[:, 0:G], in_=xT[:, kc * G:(kc + 1) * G], func=AF.Square), [XN(kc)], [sn])
                op("pe", lambda e, kc=kc, sq=sq: e.matmul(bank[:, 0:G], lhsT=(ones_b[:] if STOP.get("F", 1) else ones_f), rhs=sq[:, 0:G], start=(kc == 0), stop=(kc == KC - 1)),
                   [sn, "ones_b", "consts"], [bn])
            rs, rn = tmp()
            op("act", lambda e: e.activation(out=rs[:, 0:G], in_=bank[:, 0:G], func=AF.Sqrt, bias=eps_ap, scale=1.0), [bn, "consts"], [rn])
            op("dve", lambda e: e.reciprocal(out=rs[:, 0:G], in_=rs[:, 0:G]), [rn], [rn])
            for kc in range(KC):
                op("dve", lambda e, kc=kc: e.scalar_tensor_tensor(out=xT[:, kc * G:(kc + 1) * G], in0=xT[:, kc * G:(kc + 1) * G],
                                                                 scalar=smalls[:, O_FG + kc:O_FG + kc + 1], in1=rs[:, 0:G],
                                                                 op0=ALU.mult, op1=ALU.mult), [XN(kc), rn, "smalls"], [XN(kc)])
            for b in range(4):
                for hf in range(2):
                    bank, bn = gbank()
                    rd = [XN(hf * 4 + cc) for cc in range(4)] + ["consts"]
                    S_.begin("pe", rd, [bn])
                    inst = None
                    for cc in range(4):
                        kc = hf * 4 + cc
                        inst = nc.tensor.transpose(out=bank[:, cc * 128:(cc + 1) * 128], in_=xT[:, kc * G + b * 128:kc * G + (b + 1) * 128], identity=ident_f)
                    S_.end("pe", inst, rd, [bn])
                    tb, tn = tmp()
                    evac(tb[:, 0:512], bank[:, 0:512], [bn], [tn])
                    S_.dma("pool", "x", out=out_d[t0 + b * 128:t0 + (b + 1) * 128, hf * 512:(hf + 1) * 512], in_=tb[:, 0:512],
                           reads=[tn], writes=["outd"])

    per_g = (48 + TG - 1) // TG
    prepare_x(0, 0, 0)
    norm_stats(xTb[0], (lambda kc: "xT0_%d" % kc), rsn, "rsn")
    for l in range(L):
        for g in range(TG):
            cis = list(range(g * per_g, min(48, (g + 1) * per_g))) if l + 1 < L else []
            pre = 0 < len(cis) <= 6
            if pre:
                c1, c2 = cis[:3], cis[3:]
                emit_mod_load(l + 1, c1)

                def hook(l=l, c1=c1, c2=c2):
                    emit_mod_mm(l + 1, c1)
                    emit_mod_load(l + 1, c2)
                layer_group(l, g, hook)
                emit_mod_mm(l + 1, c2)
            else:
                layer_group(l, g)
                if l + 1 < L:
                    emit_mod(l + 1, cis)
            if g == 0 and l + 1 < L:
                emit_convert(l + 1)
        if l + 1 < L:
            emit_der(l + 1)
    if dbg_d is not None:
        S_.dma("sp", "m", out=dbg_d[:, 0:48], in_=modv[:, 0:48], reads=["modv0"], writes=["dbgd"])
        S_.dma("sp", "m", out=dbg_d[:, 48:64], in_=der[:, 0:16], reads=["der0"], writes=["dbgd2"])
    S_.wait_all("pool")
    S_.wait_all("sp")
    es.close()
    return nc


def make_consts():
    c = np.zeros((P, NCON), np.float32)
    c[:, C_ID:C_ID + 128] = np.eye(128, dtype=np.float32)
    c[:, C_ONES:C_ONES + 128] = 1.0 / 1024.0
    si = np.arange(128)[:, None]
    qi = np.arange(128)[None, :]
    c[:, C_TRI:C_TRI + 128] = (si <= qi).astype(np.float32)
    for h in range(4):
        slope = 2.0 ** (-8.0 * (h + 1) / 4)
        for di in range(ND):
            dist = di - 3
            c[:, C_DB + h * ND + di] = slope * (np.arange(128) - 127) - slope * 128.0 * dist
    c[:, C_EPS] = EPS
    sw = np.zeros((P, 2, 2, 4, 128), np.float32)
    for which in range(2):
        for half in range(2):
            for cc in range(4):
                h = 2 * cc + half
                slope = 2.0 ** (-(h + 1))
                if which == 1:
                    delta = qi - si
                    valid = delta >= 0
                else:
                    delta = qi + 128 - si
                    valid = delta < 128
                sw[:, which, half, cc, :] = np.where(valid, -slope * delta, -30000.0)
    return c, sw.reshape(P, 2048)


def pm(v):
    v = np.asarray(v, np.float32)
    return np.ascontiguousarray(v.reshape(-1, P).T)


def make_smalls(b, inp, L):
    NS = L * LS + 16 + 8
    s = np.zeros((P, NS), np.float32)
    for l in range(L):
        o = l * LS
        s[:, o + O_G1:o + O_G1 + 8] = pm(inp["norm1_g"][l])
        s[:, o + O_G2:o + O_G2 + 8] = pm(inp["norm2_g"][l])
        for k in range(3):
            s[:, o + O_CW + k * NJ:o + O_CW + (k + 1) * NJ] = pm(inp["conv_w"][l, k])
        s[:, o + O_CB:o + O_CB + NJ] = pm(inp["conv_b"][l])
        s[:, o + O_BADA:o + O_BADA + 48] = pm(inp["b_ada"][l])
        s[:, o + O_SINK:o + O_SINK + 8] = np.broadcast_to(np.asarray(inp["swa_sinks"][l], np.float32)[None, :], (P, 8))
        s[:, o + O_LAM:o + O_LAM + 256] = np.broadcast_to(np.asarray(inp["diff_lambda"][l], np.float32).reshape(1, 256), (P, 256))
        s[:, o + O_SUBG] = np.asarray(inp["diff_subln_g"][l], np.float32)
    cc = pm(inp["c"][b])
    s[:, L * LS:L * LS + 16] = np.repeat(cc, 2, axis=1)
    s[:, L * LS + 16:L * LS + 24] = pm(inp["final_g"])
    return s


_CACHE = {}


def run(inp, S, L, n_cores):
    key = (S, L)
    if key not in _CACHE:
        _CACHE[key] = build(S, L)
    nc = _CACHE[key]
    consts, swab = make_consts()
    f = lambda a: np.ascontiguousarray(np.asarray(a, np.float32))
    shared = {
        "consts": consts, "swab": swab,
        "w_ada": f(inp["w_ada"][:L]), "w_in": f(inp["w_in"][:L]), "w_bs": f(inp["w_branch_swa"][:L]),
        "w_bd": f(inp["w_branch_diff"][:L]), "w_out": f(inp["w_out"][:L]), "w_up": f(inp["w_up"][:L]),
        "w_dn": f(inp["w_down"][:L]),
    }
    real_slots = [0, 1, 4, 5][:n_cores] if (n_cores == 4 and not STOP.get("nodummy")) else list(range(n_cores))
    n_launch = 8 if real_slots != list(range(n_cores)) else n_cores
    zero_shared = None
    in_maps = []
    for slot in range(n_launch):
        if slot in real_slots:
            b = real_slots.index(slot)
            m = dict(shared)
            m["x"] = f(inp["x"][b, :S])
            m["smalls"] = make_smalls(b, inp, L)
        else:
            if zero_shared is None:
                zero_shared = {k: np.zeros_like(v) for k, v in shared.items()}
                zero_shared["consts"] = consts
                zero_shared["swab"] = swab
            m = dict(zero_shared)
            m["x"] = np.zeros((S, D), np.float32)
            m["smalls"] = np.zeros((P, L * LS + 24), np.float32)
        in_maps.append(m)
    res = run_bass_kernel_spmd(nc, in_maps, core_ids=list(range(n_launch)))
    if STOP.get("dbg"):
        STOP["dbg_out"] = [np.asarray(r["dbg"]) for r in res.results]
    return np.stack([np.asarray(res.results[slot]["out"], np.float32) for slot in real_slots], axis=0)


def kernel(**inputs):
    inp = {k: np.asarray(v) for k, v in inputs.items()}
    B, S, _ = inp["x"].shape
    L = inp["w_in"].shape[0]
    return run(inp, S, L, B)
```

```python
import math
from contextlib import ExitStack
import numpy as np
import concourse.bass as bass
import concourse.mybir as mybir
from concourse.bass_utils import run_bass_kernel_spmd

F32 = mybir.dt.float32
BF16 = mybir.dt.bfloat16
AF = mybir.ActivationFunctionType
ALU = mybir.AluOpType
AX = mybir.AxisListType

P = 128
D = 1024
KC = 8
DFF = 2816
NJ = 22
INC = 4352
G = 512
EPS = 1e-6
ND = 36
NBUF = 6
USZ = 2048
NTMP = 7
NPT = 6

O_G1, O_G2, O_CW, O_CB, O_BADA, O_SINK, O_LAM, O_SUBG = 0, 8, 16, 82, 104, 152, 160, 416
LS = 417
C_ID, C_ONES, C_TRI, C_DB, C_EPS = 0, 128, 256, 384, 384 + 4 * ND
NCON = C_EPS + 1


STOP = {}


def lam_init_of(l):
    return 0.8 - 0.6 * math.exp(-0.3 * l)


class Sched:
    def __init__(self, nc):
        self.nc = nc
        self.eng = {}
        self.semh = {}
        for name, h in [("pe", nc.tensor), ("act", nc.scalar), ("dve", nc.vector),
                        ("pool", nc.gpsimd), ("sp", nc.sync)]:
            sem = nc.alloc_semaphore(name="s_" + name)
            self.eng[name] = dict(h=h, sem=sem, cnt=0, waited={})
            self.semh[name] = sem
        self.res = {}
        self.rings = {}
        self.n_wait = 0

    def add_ring(self, rname, n):
        lst = []
        for i in range(n):
            key = "d_%s_%d" % (rname, i)
            sem = self.nc.alloc_semaphore(name=key)
            self.semh[key] = sem
            lst.append(dict(key=key, sem=sem, val=0))
        self.rings[rname] = dict(lst=lst, rr=0)

    def _wait(self, ename, tok):
        key, val = tok
        e = self.eng[ename]
        if e["waited"].get(key, 0) >= val:
            return
        e["h"].wait_ge(self.semh[key], val)
        e["waited"][key] = val
        self.n_wait += 1

    def begin(self, ename, reads=(), writes=()):
        deps = []
        for r in reads:
            st = self.res.get(r)
            if st:
                deps += st["w"]
        for w in writes:
            st = self.res.get(w)
            if st:
                deps += st["w"] + list(st["r"].values())
        for tok in deps:
            if ename == "pe" and tok[0] == "pe":
                continue
            self._wait(ename, tok)

    def _record(self, tok, reads, writes):
        for r in reads:
            st = self.res.setdefault(r, dict(w=[], r={}))
            st["r"][tok[0]] = tok
        for w in writes:
            self.res[w] = dict(w=[tok], r={})

    def end(self, ename, inst, reads=(), writes=()):
        e = self.eng[ename]
        e["cnt"] += 1
        inst.then_inc(e["sem"], 1)
        tok = (ename, e["cnt"])
        self._record(tok, reads, writes)
        return tok

    def op(self, ename, fn, reads=(), writes=()):
        self.begin(ename, reads, writes)
        inst = fn(self.eng[ename]["h"])
        return self.end(ename, inst, reads, writes)

    def dma(self, q, ring, out, in_, reads=(), writes=()):
        rg = self.rings[ring]
        slot = rg["lst"][rg["rr"] % len(rg["lst"])]
        rg["rr"] += 1
        if slot["val"] > 0:
            self._wait(q, (slot["key"], slot["val"]))
        self.begin(q, reads, writes)
        slot["val"] += 16
        self.eng[q]["h"].dma_start(out=out, in_=in_).then_inc(slot["sem"], 16)
        tok = (slot["key"], slot["val"])
        self._record(tok, reads, writes)
        return tok

    def wait_all(self, ename):
        for st in self.res.values():
            for tok in st["w"] + list(st["r"].values()):
                self._wait(ename, tok)


def build(S, L):
    nc = bass.Bass("TRN2", target_bir_lowering=False)
    TG = S // G
    NBS = S // P
    NS = L * LS + 16 + 8
    O_C = L * LS
    O_FG = L * LS + 16
    dt = nc.dram_tensor
    x_d = dt("x", [S, D], F32, kind="ExternalInput").ap()
    smalls_d = dt("smalls", [P, NS], F32, kind="ExternalInput").ap()
    consts_d = dt("consts", [P, NCON], F32, kind="ExternalInput").ap()
    swab_d = dt("swab", [P, 2048], F32, kind="ExternalInput").ap()
    wada_d = dt("w_ada", [L, D, 6 * D], F32, kind="ExternalInput").ap()
    win_d = dt("w_in", [L, D, INC], F32, kind="ExternalInput").ap()
    wbs_d = dt("w_bs", [L, 512, D], F32, kind="ExternalInput").ap()
    wbd_d = dt("w_bd", [L, 512, D], F32, kind="ExternalInput").ap()
    wout_d = dt("w_out", [L, D, D], F32, kind="ExternalInput").ap()
    wup_d = dt("w_up", [L, D, 2 * DFF], F32, kind="ExternalInput").ap()
    wdn_d = dt("w_dn", [L, DFF, D], F32, kind="ExternalInput").ap()
    out_d = dt("out", [S, D], F32, kind="ExternalOutput").ap()
    dbg_d = dt("dbg", [P, 64], F32, kind="ExternalOutput").ap() if STOP.get("dbg") else None
    xT_d = dt("xT_scr", [D, S], F32, kind="Internal").ap()
    winb = [dt("winb%d" % l, [D, INC], BF16, kind="Internal").ap() for l in range(L)]
    wbsb = [dt("wbsb%d" % l, [512, D], BF16, kind="Internal").ap() for l in range(L)]
    wbdb = [dt("wbdb%d" % l, [512, D], BF16, kind="Internal").ap() for l in range(L)]
    woutb = [dt("woutb%d" % l, [D, D], BF16, kind="Internal").ap() for l in range(L)]
    wupb = [dt("wupb%d" % l, [D, 2 * DFF], BF16, kind="Internal").ap() for l in range(L)]
    wdnb = [dt("wdnb%d" % l, [DFF, D], BF16, kind="Internal").ap() for l in range(L)]

    es = ExitStack()
    sb = lambda name, shape, dtype: es.enter_context(nc.sbuf_tensor("sb_" + name, shape, dtype))
    xTb = [sb("xTbuf%d" % i, [P, KC * G], F32) for i in range(2)]
    hT = sb("hT", [P, KC * G], BF16)
    QM = sb("QM", [P, 8 * G], BF16)
    KdT = sb("KdT", [P, 4 * S], BF16)
    VdE = sb("VdE", [P, NBS * 4 * 130], BF16)
    KaT = sb("KaT", [P, 2 * 640], BF16)
    VaE = sb("VaE", [P, 5 * 132], BF16)
    oa_tok = sb("oa_tok", [P, 512], BF16)
    od_tok = [sb("od_tok%d" % i, [P, 128], BF16) for i in range(4)]
    oaT = sb("oaT", [P, 4 * G], BF16)
    odT = sb("odT", [P, 4 * G], BF16)
    ring = [sb("ring%d" % i, [P, USZ], BF16) for i in range(NBUF)]
    actT = [sb("actT%d" % i, [P, 4 * G], BF16) for i in range(2)]
    tmps = [sb("tmp%d" % i, [P, 520], F32) for i in range(NTMP)]
    PTs = [sb("PT%d" % i, [P, 512], BF16) for i in range(NPT)]
    sigs = [sb("sig%d" % i, [P, 512], BF16) for i in range(2)]
    adabig = sb("adabig", [P, KC * 384], BF16)
    consts = sb("consts", [P, NCON], F32)
    swaB = sb("swaB", [P, 2048], BF16)
    smalls = sb("smalls", [P, NS], F32)
    ident_b = sb("ident_b", [P, 128], BF16)
    ones_b = sb("ones_b", [P, 128], BF16)
    sqb = [sb("sqb%d" % i, [P, 512], BF16) for i in range(2)]
    tri_b = sb("tri_b", [P, 128], BF16)
    cs2 = sb("cs2", [P, 16], BF16)
    modv = sb("modv", [P, L * 48], F32)
    der = sb("der", [P, L * 16], F32)
    misc = sb("misc", [P, L * 16], F32)
    halo = sb("halo", [P, NJ * 2], F32)
    rsn = sb("rsn", [P, G], F32)
    ps = [es.enter_context(nc.psum_tensor("ps%d" % i, [P, 512], F32)) for i in range(8)]

    S_ = Sched(nc)
    S_.add_ring("w", NBUF)
    S_.add_ring("x", 16)
    S_.add_ring("cv", 8)
    S_.add_ring("ada", 4)
    S_.add_ring("m", 4)
    op = S_.op

    ident_f = consts[:, C_ID:C_ID + 128]
    ones_f = consts[:, C_ONES:C_ONES + 128]
    eps_ap = consts[:, C_EPS:C_EPS + 1]

    st = dict(gb=0, tmp=0, pt=0, sig=0, ring=0, od=0, ada=0, ev=0, sq=0, banks=list(range(STOP.get("nb", 8))))

    def gbank():
        pool_ = st["banks"]
        i = pool_[st["gb"] % len(pool_)]
        st["gb"] += 1
        return ps[i], "ps%d" % i

    def tmp():
        i = st["tmp"] % NTMP
        st["tmp"] += 1
        return tmps[i], "tmp%d" % i

    def sqbuf():
        i = st["sq"] % 2
        st["sq"] += 1
        return sqb[i], "sqb%d" % i

    def ptbuf():
        i = st["pt"] % NPT
        st["pt"] += 1
        return PTs[i], "PT%d" % i

    def sigbuf():
        i = st["sig"] % 2
        st["sig"] += 1
        return sigs[i], "sig%d" % i

    def evac(out, in_, reads, writes, eng=None):
        if eng is None:
            bk = [r for r in reads if r.startswith("ps")]
            eng = "act" if (bk and int(bk[0][2:]) % 2 == 0) else "dve"
        if eng == "act":
            return op("act", lambda e: e.activation(out=out, in_=in_, func=AF.Copy), reads, writes)
        return op(eng, lambda e: e.tensor_copy(out=out, in_=in_), reads, writes)

    def mm_group(out, pairs, reads, writes):
        S_.begin("pe", reads, writes)
        n = len(pairs)
        inst = None
        for i, (l_, r_) in enumerate(pairs):
            inst = nc.tensor.matmul(out, lhsT=l_, rhs=r_, start=(i == 0), stop=(i == n - 1))
        return S_.end("pe", inst, reads, writes)

    def load_unit(src, kcn, ncols, reads):
        i = st["ring"] % NBUF
        st["ring"] += 1
        dst = ring[i][:, 0:kcn * ncols].rearrange("p (k n) -> p k n", n=ncols)
        S_.dma("sp", "w", out=dst, in_=src, reads=reads, writes=["ring%d" % i])
        return ring[i], "ring%d" % i

    def wsrc(w, r0, nr, c0, ncols):
        return w[r0:r0 + nr, c0:c0 + ncols].rearrange("(kc p) n -> p kc n", p=P)

    S_.dma("sp", "m", out=consts[:], in_=consts_d[:, :], writes=["consts"])
    S_.dma("sp", "m", out=smalls[:], in_=smalls_d[:, :], writes=["smalls"])
    S_.dma("pool", "m", out=swaB[:], in_=swab_d[:, :], writes=["swaB"])
    op("dve", lambda e: e.tensor_copy(out=ident_b[:], in_=consts[:, C_ID:C_ID + 128]), ["consts"], ["ident_b"])
    op("dve", lambda e: e.tensor_copy(out=tri_b[:], in_=consts[:, C_TRI:C_TRI + 128]), ["consts"], ["tri_b"])
    op("dve", lambda e: e.tensor_copy(out=ones_b[:], in_=consts[:, C_ONES:C_ONES + 128]), ["consts"], ["ones_b"])
    op("pool", lambda e: e.memset(VdE[:], 1.0), [], ["VdE%d" % b for b in range(NBS)])
    op("pool", lambda e: e.memset(VaE[:], 1.0), [], ["VaEprev", "VaEcur"])
    op("act", lambda e: e.activation(out=cs2[:, 0:16], in_=smalls[:, O_C:O_C + 16], func=AF.Silu), ["smalls"], ["cs2"])

    def emit_convert(l, parts=None):
        lst = [(winb[l], win_d[l], "winb"), (wbsb[l], wbs_d[l], "wbsb"), (wbdb[l], wbd_d[l], "wbdb"),
               (woutb[l], wout_d[l], "woutb"), (wupb[l], wup_d[l], "wupb"), (wdnb[l], wdn_d[l], "wdnb")]
        for i, (dst, src, rn) in enumerate(lst):
            if parts is not None and i not in parts:
                continue
            S_.dma("pool", "cv", out=dst[:, :], in_=src, writes=["%s%d" % (rn, l)])

    def emit_mod_load(l, cis):
        cis = list(cis)
        n = len(cis)
        if n == 0:
            return
        S_.dma("pool", "ada", out=adabig[:, 0:KC * n * 128].rearrange("p (k n) -> p k n", n=n * 128),
               in_=wsrc(wada_d[l], 0, D, cis[0] * 128, n * 128), writes=["adabig"])

    def emit_mod_mm(l, cis):
        cis = list(cis)
        n = len(cis)
        for ii, ci in enumerate(cis):
            bank, bn = gbank()
            mm_group(bank[:, 0:2], [(adabig[:, kc * n * 128 + ii * 128:kc * n * 128 + (ii + 1) * 128], cs2[:, 2 * kc:2 * kc + 2]) for kc in range(KC)],
                     ["adabig", "cs2"], [bn])
            op("dve", lambda e, ci=ci, bank=bank: e.tensor_tensor(
                out=modv[:, l * 48 + ci:l * 48 + ci + 1], in0=bank[:, 0:1],
                in1=smalls[:, l * LS + O_BADA + ci:l * LS + O_BADA + ci + 1], op=ALU.add),
               [bn, "smalls"], ["modv%d" % l])

    def emit_mod(l, cis):
        cis = list(cis)
        for k in range(0, len(cis), 3):
            emit_mod_load(l, cis[k:k + 3])
            emit_mod_mm(l, cis[k:k + 3])

    def emit_der(l):
        for (o0, m0, g0) in [(0, 8, O_G1), (8, 32, O_G2)]:
            op("dve", lambda e, o0=o0, m0=m0, g0=g0: e.scalar_tensor_tensor(
                out=der[:, l * 16 + o0:l * 16 + o0 + 8], in0=modv[:, l * 48 + m0:l * 48 + m0 + 8], scalar=1.0,
                in1=smalls[:, l * LS + g0:l * LS + g0 + 8], op0=ALU.add, op1=ALU.mult),
               ["modv%d" % l, "smalls"], ["der%d" % l])

    def emit_misc(l):
        b = l * 16
        li = lam_init_of(l)
        lo = l * LS + O_LAM
        op("act", lambda e: e.activation(out=misc[:, b:b + 8], in_=smalls[:, l * LS + O_SINK:l * LS + O_SINK + 8], func=AF.Exp),
           ["smalls"], ["misc"])
        t, tn = tmp()
        for k in range(2):
            op("dve", lambda e, k=k: e.tensor_tensor(out=t[:, k * 64:(k + 1) * 64], in0=smalls[:, lo + k * 128:lo + k * 128 + 64],
                                                     in1=smalls[:, lo + k * 128 + 64:lo + k * 128 + 128], op=ALU.mult), ["smalls"], [tn])
            op("dve", lambda e, k=k: e.tensor_reduce(out=misc[:, b + 10 + k:b + 11 + k], in_=t[:, k * 64:(k + 1) * 64], axis=AX.X, op=ALU.add),
               [tn], ["misc"])
        op("act", lambda e: e.activation(out=misc[:, b + 12:b + 14], in_=misc[:, b + 10:b + 12], func=AF.Exp), ["misc"], ["misc"])
        op("dve", lambda e: e.tensor_tensor(out=misc[:, b + 8:b + 9], in0=misc[:, b + 13:b + 14], in1=misc[:, b + 12:b + 13], op=ALU.subtract),
           ["misc"], ["misc"])
        op("dve", lambda e: e.tensor_scalar(out=misc[:, b + 8:b + 9], in0=misc[:, b + 8:b + 9], scalar1=-li, scalar2=None, op0=ALU.add),
           ["misc"], ["misc"])
        op("dve", lambda e: e.tensor_scalar(out=misc[:, b + 9:b + 10], in0=smalls[:, l * LS + O_SUBG:l * LS + O_SUBG + 1],
                                            scalar1=(1.0 - li), scalar2=None, op0=ALU.mult), ["smalls", "misc"], ["misc"])

    emit_convert(0)
    for l in range(L):
        emit_misc(l)
    emit_mod(0, range(48))
    emit_der(0)

    def norm_stats(xT, XN, rs, rn):
        bank, bn = gbank()
        for kc in range(KC):
            sq, sn = (sqbuf() if STOP.get("F", 1) else tmp())
            op("act", lambda e, kc=kc, sq=sq: e.activation(out=sq[:, 0:G], in_=xT[:, kc * G:(kc + 1) * G], func=AF.Square),
               [XN(kc)], [sn])
            op("pe", lambda e, kc=kc, sq=sq: e.matmul(bank[:, 0:G], lhsT=(ones_b[:] if STOP.get("F", 1) else ones_f), rhs=sq[:, 0:G], start=(kc == 0), stop=(kc == KC - 1)),
               [sn, "ones_b", "consts"], [bn])
        op("act", lambda e: e.activation(out=rs[:, 0:G], in_=bank[:, 0:G], func=AF.Sqrt, bias=eps_ap, scale=1.0), [bn, "consts"], [rn])
        op("dve", lambda e: e.reciprocal(out=rs[:, 0:G], in_=rs[:, 0:G]), [rn], [rn])

    def norm_to_hT(l, which, xT, XN, pre_rs=None):
        if pre_rs is None:
            rs, rn = tmp()
            norm_stats(xT, XN, rs, rn)
        else:
            rs, rn = pre_rs
        so = l * 16 + (0 if which == 1 else 8)
        sho = l * 48 + (0 if which == 1 else 24)
        for kc in range(KC):
            t, tn = tmp()
            if tn == rn:
                t, tn = tmp()
            op("dve", lambda e, kc=kc, t=t: e.scalar_tensor_tensor(out=t[:, 0:G], in0=xT[:, kc * G:(kc + 1) * G],
                                                                  scalar=der[:, so + kc:so + kc + 1], in1=rs[:, 0:G],
                                                                  op0=ALU.mult, op1=ALU.mult),
               [XN(kc), rn, "der%d" % l], [tn])
            op("act", lambda e, kc=kc, t=t: e.activation(out=hT[:, kc * G:(kc + 1) * G], in_=t[:, 0:G], func=AF.Identity,
                                                         bias=modv[:, sho + kc:sho + kc + 1], scale=1.0),
               [tn, "modv%d" % l], ["hT%d" % kc])

    HT_ALL = ["hT%d" % kc for kc in range(KC)]

    def prepare_x(l, g, p):
        xT = xTb[p]
        XN = lambda kc: "xT%d_%d" % (p, kc)
        t0 = g * G
        if l == 0:
            for b in range(4):
                for hf in range(2):
                    tb, tn = tmp()
                    S_.dma("pool", "x", out=tb[:, 0:512], in_=x_d[t0 + b * 128:t0 + (b + 1) * 128, hf * 512:(hf + 1) * 512], writes=[tn])
                    bank, bn = gbank()
                    S_.begin("pe", [tn, "consts"], [bn])
                    inst = None
                    for cc in range(4):
                        inst = nc.tensor.transpose(out=bank[:, cc * 128:(cc + 1) * 128], in_=tb[:, cc * 128:(cc + 1) * 128], identity=ident_f)
                    S_.end("pe", inst, [tn, "consts"], [bn])
                    outv = xT[:, hf * 4 * G:(hf + 1) * 4 * G].rearrange("p (c t) -> p c t", t=G)[:, :, b * 128:(b + 1) * 128]
                    evac(outv, bank[:, 0:512].rearrange("p (c t) -> p c t", t=128), [bn], [XN(hf * 4 + cc) for cc in range(4)])
        else:
            for kc in range(KC):
                S_.dma("pool", "x", out=xT[:, kc * G:(kc + 1) * G], in_=xT_d[kc * P:(kc + 1) * P, t0:t0 + G],
                       reads=["xTd%d_%d" % (g, kc)], writes=[XN(kc)])

    def layer_group(l, g, mid_hook=None):
        t0 = g * G
        last_layer = (l == L - 1)
        p = (l * TG + g) % 2
        xT = xTb[p]
        XN = lambda kc: "xT%d_%d" % (p, kc)
        if STOP.get('s', 99) == 1:
            return
        norm_to_hT(l, 1, xT, XN, pre_rs=(rsn, "rsn"))
        if STOP.get('s', 99) == 2:
            return
        wl = "winb%d" % l
        for uh in range(2):
            u, un = load_unit(wsrc(winb[l], 0, D, 1792 + uh * 256, 256), KC, 256, [wl])
            for bp in range(2):
                bank, bn = gbank()
                for bb in range(2):
                    b = bp * 2 + bb
                    mm_group(bank[:, bb * 256:(bb + 1) * 256],
                             [(hT[:, kc * G + b * 128:kc * G + (b + 1) * 128], u[:, kc * 256:(kc + 1) * 256]) for kc in range(KC)],
                             HT_ALL + [un], [bn])
                if STOP.get('s', 99) == 211:
                    return
                for bb in range(2):
                    b = bp * 2 + bb
                    blk = 4 * g + b
                    o0 = (blk * 4 + 2 * uh) * 130
                    for hh in range(2):
                        evac(VdE[:, o0 + hh * 130:o0 + hh * 130 + 128], bank[:, bb * 256 + hh * 128:bb * 256 + (hh + 1) * 128], [bn], ["VdE%d" % blk])
        if STOP.get('s', 99) == 21:
            return
        u, un = load_unit(wsrc(winb[l], 0, D, 512, 256), KC, 256, [wl])
        i = st["ring"] % NBUF
        st["ring"] += 1
        usw, uswn = ring[i], "ring%d" % i
        uswv = usw[:, 0:1024].rearrange("p (k n) -> p k n", n=128)
        S_.dma("sp", "w", out=uswv[:, :, 0:64], in_=wsrc(winb[l], 0, D, 576, 64), reads=[wl], writes=[uswn])
        S_.dma("sp", "w", out=uswv[:, :, 64:128], in_=wsrc(winb[l], 0, D, 512, 64), reads=[wl, uswn], writes=[uswn])
        for r in range(2):
            bank, bn = gbank()
            if r == 0:
                prs = [(u[:, kc * 256:kc * 256 + 128], hT[:, kc * G:(kc + 1) * G]) for kc in range(KC)]
            else:
                prs = [(usw[:, kc * 128:(kc + 1) * 128], hT[:, kc * G:(kc + 1) * G]) for kc in range(KC)]
            mm_group(bank[:, 0:G], prs, HT_ALL + [un, uswn], [bn])
            evac(KaT[:, r * 640 + 128:r * 640 + 640], bank[:, 0:G], [bn], ["KaTcur"])
        if STOP.get('s', 99) == 22:
            return
        bank, bn = gbank()
        for b in range(4):
            mm_group(bank[:, b * 128:(b + 1) * 128],
                     [(hT[:, kc * G + b * 128:kc * G + (b + 1) * 128], u[:, kc * 256 + 128:kc * 256 + 256]) for kc in range(KC)],
                     HT_ALL + [un], [bn])
        for b in range(4):
            for k in range(2):
                evac(VaE[:, (b + 1) * 132 + k * 66:(b + 1) * 132 + k * 66 + 64], bank[:, b * 128 + k * 64:b * 128 + (k + 1) * 64], [bn], ["VaEcur"])
        if STOP.get('s', 99) == 23:
            return
        for (c_base, kind) in [(1280, "kd"), (768, "qd"), (0, "qa")]:
            for uh in range(2):
                u, un = load_unit(wsrc(winb[l], 0, D, c_base + uh * 256, 256), KC, 256, [wl])
                for cc in range(2):
                    ch = uh * 2 + cc
                    bank, bn = gbank()
                    mm_group(bank[:, 0:G], [(u[:, kc * 256 + cc * 128:kc * 256 + (cc + 1) * 128], hT[:, kc * G:(kc + 1) * G]) for kc in range(KC)],
                             HT_ALL + [un], [bn])
                    if kind == "kd":
                        evac(KdT[:, ch * S + t0:ch * S + t0 + G], bank[:, 0:G], [bn], ["KdT%d_%d" % (ch, g)])
                    elif kind == "qd":
                        evac(QM[:, ch * G:(ch + 1) * G], bank[:, 0:G], [bn], ["QM%d" % ch])
                    else:
                        evac(QM[:, (4 + ch) * G:(5 + ch) * G], bank[:, 0:G], [bn], ["QM%d" % (4 + ch)])
        if STOP.get('s', 99) == 3:
            return
        g2 = (g + 1) % TG
        l2 = l if g + 1 < TG else l + 1
        if l2 < L:
            prepare_x(l2, g2, 1 - p)
        st["banks"] = [7]
        QA = ["QM%d" % c for c in range(4, 8)]
        for b in range(4):
            blk = 4 * g + b
            pts = {}
            for which in (0, 1):
                if which == 0 and blk == 0:
                    continue
                if which == 1:
                    kc0, kres = 128 + b * 128, ["KaTcur"]
                elif b == 0:
                    kc0, kres = 0, ["KaTprev"]
                else:
                    kc0, kres = 128 + (b - 1) * 128, ["KaTcur"]
                for half in range(2):
                    bi = half + 2 * which
                    bank, bn = ps[bi], "ps%d" % bi
                    S_.begin("pe", kres + QA, [bn])
                    inst = None
                    for c in range(4):
                        h = 2 * c + half
                        kv = h // 4
                        inst = nc.tensor.matmul(bank[:, c * 128:(c + 1) * 128],
                                                lhsT=KaT[half * 64:(half + 1) * 64, (0 if kv == half else 1) * 640 + kc0:(0 if kv == half else 1) * 640 + kc0 + 128],
                                                rhs=QM[half * 64:(half + 1) * 64, (4 + c) * G + b * 128:(4 + c) * G + (b + 1) * 128],
                                                start=True, stop=True)
                    S_.end("pe", inst, kres + QA, [bn])
                    sc, scn = tmp()
                    bo = (which * 2 + half) * 512
                    op("dve", lambda e, sc=sc, bank=bank, bo=bo: e.scalar_tensor_tensor(
                        out=sc[:, 0:512], in0=bank[:, 0:512], scalar=0.125, in1=swaB[:, bo:bo + 512], op0=ALU.mult, op1=ALU.add),
                       [bn, "swaB"], [scn])
                    pt, ptn = ptbuf()
                    op("act", lambda e, sc=sc, pt=pt: e.activation(out=pt[:, 0:512], in_=sc[:, 0:512], func=AF.Exp), [scn], [ptn])
                    pts[(which, half)] = (pt, ptn)
            if STOP.get('s', 99) == 31:
                return
            whichs = [w_ for w_ in (0, 1) if (w_, 0) in pts]
            rd = [pts[k][1] for k in pts] + ["VaEprev", "VaEcur"]
            S_.begin("pe", rd, ["ps4", "ps5"])
            inst = None
            for h in range(8):
                half, c, kv = h % 2, h // 2, h // 4
                reg = ps[4 + h // 4][:, (h % 4) * 65:(h % 4) * 65 + 65]
                for i, w_ in enumerate(whichs):
                    vb = b + w_
                    inst = nc.tensor.matmul(reg, lhsT=pts[(w_, half)][0][:, c * 128:(c + 1) * 128],
                                            rhs=VaE[:, vb * 132 + kv * 66:vb * 132 + kv * 66 + 65],
                                            start=(i == 0), stop=(i == len(whichs) - 1))
            S_.end("pe", inst, rd, ["ps4", "ps5"])
            if STOP.get('s', 99) == 32:
                return
            sm, smn = tmp()
            S_.begin("dve", ["ps4", "ps5", "misc"], [smn])
            inst = None
            for h in range(8):
                inst = nc.vector.tensor_tensor(out=sm[:, h:h + 1], in0=ps[4 + h // 4][:, (h % 4) * 65 + 64:(h % 4) * 65 + 65],
                                               in1=misc[:, l * 16 + h:l * 16 + h + 1], op=ALU.add)
            S_.end("dve", inst, ["ps4", "ps5", "misc"], [smn])
            op("dve", lambda e, sm=sm: e.reciprocal(out=sm[:, 8:16], in_=sm[:, 0:8]), [smn], [smn])
            S_.begin("dve", [smn, "ps4", "ps5"], ["oa_tok"])
            inst = None
            for h in range(8):
                inst = nc.vector.tensor_scalar(out=oa_tok[:, h * 64:(h + 1) * 64], in0=ps[4 + h // 4][:, (h % 4) * 65:(h % 4) * 65 + 64],
                                               scalar1=sm[:, 8 + h:9 + h], scalar2=None, op0=ALU.mult)
            S_.end("dve", inst, [smn, "ps4", "ps5"], ["oa_tok"])
            if STOP.get('s', 99) == 33:
                return
            tbank, tbn = ps[6], "ps6"
            S_.begin("pe", ["oa_tok", "ident_b"], [tbn])
            inst = None
            for c in range(4):
                inst = nc.tensor.matmul(tbank[:, c * 128:(c + 1) * 128], lhsT=oa_tok[:, c * 128:(c + 1) * 128], rhs=ident_b[:], start=True, stop=True)
            S_.end("pe", inst, ["oa_tok", "ident_b"], [tbn])
            if STOP.get('s', 99) == 335:
                return
            for c in range(4):
                if STOP.get('v') == 1:
                    evac(PTs[c][:, 0:128], tbank[:, c * 128:(c + 1) * 128], [tbn], ["PT%d" % c])
                elif STOP.get('v') == 2:
                    evac(oaT[:, c * G + b * 128:c * G + (b + 1) * 128], tbank[:, c * 128:(c + 1) * 128], [tbn], ["oaT%d" % c], eng="dve")
                elif STOP.get('v') == 3:
                    evac(oaT[:, c * G + b * 128:c * G + (b + 1) * 128], tbank[:, c * 128:(c + 1) * 128], [tbn], ["oaT%d" % c], eng="act")
                else:
                    evac(oaT[:, c * G + b * 128:c * G + (b + 1) * 128], tbank[:, c * 128:(c + 1) * 128], [tbn], ["oaT%d" % c], eng="dve")
            if STOP.get('s', 99) == 34:
                return
        if g < TG - 1:
            for r in range(2):
                op("pool", lambda e, r=r: e.tensor_copy(out=KaT[:, r * 640:r * 640 + 128], in_=KaT[:, r * 640 + 512:r * 640 + 640]),
                   ["KaTcur"], ["KaTprev"])
            op("pool", lambda e: e.tensor_copy(out=VaE[:, 0:132], in_=VaE[:, 528:660]), ["VaEcur"], ["VaEprev"])
        if STOP.get('s', 99) == 4:
            return
        nkb = 4 * g + 4
        for h in range(4):
            W = 1 if h == 0 else 4
            slope_h = 2.0 ** (-2.0 * (h + 1))
            dskip = int(math.ceil((200.0 / slope_h + 127.0) / 128.0))
            kb0 = max(0, 4 * g - dskip + 1)

            def region(m, j):
                r = m * 4 + j
                return ps[4 + r // 3][:, (r % 3) * 130:(r % 3) * 130 + 129], "ps%d" % (4 + r // 3)

            def scores(kb):
                j0 = max(0, kb - 4 * g)
                c0 = j0 * 128
                n = 512 - c0
                res = {}
                for m in range(2):
                    bi = m + 2 * (kb % 2)
                    bank, bn = ps[bi], "ps%d" % bi
                    op("pe", lambda e, m=m, bank=bank: e.matmul(
                        bank[:, 0:n], lhsT=KdT[m * 64:(m + 1) * 64, h * S + kb * 128:h * S + (kb + 1) * 128],
                        rhs=QM[m * 64:(m + 1) * 64, h * G + c0:(h + 1) * G], start=True, stop=True),
                       ["KdT%d_%d" % (h, kb // 4), "QM%d" % h], [bn])
                    pt, ptn = ptbuf()
                    S_.begin("act", [bn, "consts"], [ptn])
                    inst = None
                    if W == 4:
                        di = C_DB + h * ND + (4 * g - kb + 3)
                        inst = nc.scalar.activation(out=pt[:, 0:n], in_=bank[:, 0:n], func=AF.Exp, bias=consts[:, di:di + 1], scale=0.125)
                    else:
                        for j in range(j0, 4):
                            di = C_DB + h * ND + (4 * g + j - kb + 3)
                            a0 = j * 128 - c0
                            inst = nc.scalar.activation(out=pt[:, a0:a0 + 128], in_=bank[:, a0:a0 + 128], func=AF.Exp,
                                                        bias=consts[:, di:di + 1], scale=0.125)
                    S_.end("act", inst, [bn, "consts"], [ptn])
                    if kb >= 4 * g:
                        op("pool", lambda e, pt=pt: e.tensor_tensor(out=pt[:, 0:128], in0=pt[:, 0:128], in1=tri_b[:], op=ALU.mult),
                           [ptn, "tri_b"], [ptn])
                    res[m] = (pt, ptn, c0, j0)
                return res

            def pv(kb, sc):
                for m in range(2):
                    pt, ptn, c0, j0 = sc[m]
                    wr = sorted(set(region(m, j)[1] for j in range(j0, 4)))
                    S_.begin("pe", [ptn, "VdE%d" % kb], wr)
                    inst = None
                    for j in range(j0, 4):
                        reg, _ = region(m, j)
                        a0 = j * 128 - c0
                        vo = (kb * 4 + h) * 130
                        inst = nc.tensor.matmul(reg, lhsT=pt[:, a0:a0 + 128], rhs=VdE[:, vo:vo + 129],
                                                start=(kb == kb0 and (m, j) in ((0, 0), (0, 3), (1, 2))), stop=(kb == 4 * g + j))
                    S_.end("pe", inst, [ptn, "VdE%d" % kb], wr)

            prev = scores(kb0)
            for kb in range(kb0 + 1, nkb):
                cur = scores(kb)
                pv(kb - 1, prev)
                prev = cur
            pv(nkb - 1, prev)
            mo = l * 16
            R0 = [region(0, j) for j in range(4)]
            R1 = [region(1, j) for j in range(4)]
            SM = [tmp() for j in range(4)]
            J4 = range(4)
            for j in J4:
                sm, smn = SM[j]
                op("dve", lambda e, sm=sm, r0=R0[j][0]: e.reciprocal(out=sm[:, 0:1], in_=r0[:, 128:129]), [R0[j][1]], [smn])
            for j in J4:
                sm, smn = SM[j]
                op("dve", lambda e, sm=sm, r1=R1[j][0]: e.reciprocal(out=sm[:, 1:2], in_=r1[:, 128:129]), [R1[j][1], smn], [smn])
            for j in J4:
                sm, smn = SM[j]
                op("dve", lambda e, sm=sm: e.tensor_scalar(out=sm[:, 2:3], in0=sm[:, 1:2], scalar1=misc[:, mo + 8:mo + 9], scalar2=None, op0=ALU.mult),
                   [smn, "misc"], [smn])
            for j in J4:
                sm, smn = SM[j]
                op("dve", lambda e, sm=sm, r0=R0[j][0]: e.tensor_scalar(out=sm[:, 128:256], in0=r0[:, 0:128], scalar1=sm[:, 0:1], scalar2=None, op0=ALU.mult),
                   [R0[j][1], smn], [smn])
            for j in J4:
                sm, smn = SM[j]
                op("dve", lambda e, sm=sm, r1=R1[j][0]: e.scalar_tensor_tensor(out=sm[:, 128:256], in0=r1[:, 0:128], scalar=sm[:, 2:3], in1=sm[:, 128:256],
                                                                            op0=ALU.mult, op1=ALU.add), [R1[j][1], smn], [smn])
            for j in J4:
                sm, smn = SM[j]
                op("act", lambda e, sm=sm: e.activation(out=sm[:, 256:384], in_=sm[:, 128:256], func=AF.Square, accum_out=sm[:, 3:4]), [smn], [smn])
            for j in J4:
                sm, smn = SM[j]
                op("act", lambda e, sm=sm: e.activation(out=sm[:, 4:5], in_=sm[:, 3:4], func=AF.Sqrt, bias=eps_ap, scale=1.0 / 128.0),
                   [smn, "consts"], [smn])
            for j in J4:
                sm, smn = SM[j]
                op("dve", lambda e, sm=sm: e.reciprocal(out=sm[:, 5:6], in_=sm[:, 4:5]), [smn], [smn])
            for j in J4:
                sm, smn = SM[j]
                op("dve", lambda e, sm=sm, j=j: e.tensor_scalar(out=od_tok[j][:], in0=sm[:, 128:256], scalar1=sm[:, 5:6], scalar2=None, op0=ALU.mult),
                   [smn], ["od_tok%d" % j])
            tbank, tbn = gbank()
            rdl = ["od_tok%d" % j for j in J4] + ["ident_b"]
            S_.begin("pe", rdl, [tbn])
            inst = None
            for j in J4:
                inst = nc.tensor.matmul(tbank[:, j * 128:(j + 1) * 128], lhsT=od_tok[j][:], rhs=ident_b[:], start=True, stop=True)
            S_.end("pe", inst, rdl, [tbn])
            op("act", lambda e, tbank=tbank: e.activation(out=odT[:, h * G:(h + 1) * G], in_=tbank[:, 0:G], func=AF.Copy,
                                                        scale=misc[:, mo + 9:mo + 10]), [tbn, "misc"], ["odT%d" % h])
        if STOP.get('s', 99) == 5:
            return
        st["banks"] = list(range(STOP.get("nb", 8)))
        if mid_hook is not None:
            mid_hook()
        if l2 < L:
            norm_stats(xTb[1 - p], (lambda kc: "xT%d_%d" % (1 - p, kc)), rsn, "rsn")
        OA = ["oaT%d" % c for c in range(4)]
        OD = ["odT%d" % c for c in range(4)]
        for op2 in range(4):
            i = st["ring"] % NBUF
            st["ring"] += 1
            ub, ubn = ring[i], "ring%d" % i
            S_.dma("sp", "w", out=ub[:, 0:1024].rearrange("p (k n) -> p k n", n=256), in_=wsrc(wbsb[l], 0, 512, op2 * 256, 256),
                   reads=["wbsb%d" % l], writes=[ubn])
            S_.dma("sp", "w", out=ub[:, 1024:2048].rearrange("p (k n) -> p k n", n=256), in_=wsrc(wbdb[l], 0, 512, op2 * 256, 256),
                   reads=["wbdb%d" % l, ubn], writes=[ubn])
            uga, ugan = load_unit(wsrc(winb[l], 0, D, 2304 + op2 * 256, 256), KC, 256, [wl])
            ugd, ugdn = load_unit(wsrc(winb[l], 0, D, 3328 + op2 * 256, 256), KC, 256, [wl])
            for cc in range(2):
                oc = op2 * 2 + cc
                cs_ = slice(cc * 128, (cc + 1) * 128)
                bga, bgan = gbank()
                mm_group(bga[:, 0:G], [(uga[:, kc * 256 + cc * 128:kc * 256 + (cc + 1) * 128], hT[:, kc * G:(kc + 1) * G]) for kc in range(KC)],
                         HT_ALL + [ugan], [bgan])
                sga, sgan = sigbuf()
                op("act", lambda e, sga=sga, bga=bga: e.activation(out=sga[:, 0:G], in_=bga[:, 0:G], func=AF.Sigmoid), [bgan], [sgan])
                ba, ban = gbank()
                mm_group(ba[:, 0:G], [(ub[:, kc * 256 + cc * 128:kc * 256 + (cc + 1) * 128], oaT[:, kc * G:(kc + 1) * G]) for kc in range(4)],
                         OA + [ubn], [ban])
                t1, t1n = tmp()
                op("dve", lambda e, t1=t1, ba=ba, sga=sga: e.tensor_tensor(out=t1[:, 0:G], in0=ba[:, 0:G], in1=sga[:, 0:G], op=ALU.mult),
                   [ban, sgan], [t1n])
                bgd, bgdn = gbank()
                mm_group(bgd[:, 0:G], [(ugd[:, kc * 256 + cc * 128:kc * 256 + (cc + 1) * 128], hT[:, kc * G:(kc + 1) * G]) for kc in range(KC)],
                         HT_ALL + [ugdn], [bgdn])
                sgd, sgdn = sigbuf()
                op("act", lambda e, sgd=sgd, bgd=bgd: e.activation(out=sgd[:, 0:G], in_=bgd[:, 0:G], func=AF.Sigmoid), [bgdn], [sgdn])
                bd, bdn = gbank()
                mm_group(bd[:, 0:G], [(ub[:, 1024 + kc * 256 + cc * 128:1024 + kc * 256 + (cc + 1) * 128], odT[:, kc * G:(kc + 1) * G]) for kc in range(4)],
                         OD + [ubn], [bdn])
                t2, t2n = tmp()
                op("dve", lambda e, t2=t2, bd=bd, sgd=sgd: e.tensor_tensor(out=t2[:, 0:G], in0=bd[:, 0:G], in1=sgd[:, 0:G], op=ALU.mult),
                   [bdn, sgdn], [t2n])
                op("pool", lambda e, oc=oc, t1=t1, t2=t2: e.tensor_tensor(out=QM[:, oc * G:(oc + 1) * G], in0=t1[:, 0:G], in1=t2[:, 0:G], op=ALU.add),
                   [t1n, t2n], ["QM%d" % oc])
        if STOP.get('s', 99) == 6:
            return
        MG = ["QM%d" % c for c in range(8)]
        for op2 in range(4):
            u, un = load_unit(wsrc(woutb[l], 0, D, op2 * 256, 256), KC, 256, ["woutb%d" % l])
            for cc in range(2):
                oc = op2 * 2 + cc
                bank, bn = gbank()
                mm_group(bank[:, 0:G], [(u[:, kc * 256 + cc * 128:kc * 256 + (cc + 1) * 128], QM[:, kc * G:(kc + 1) * G]) for kc in range(KC)],
                         MG + [un], [bn])
                go = l * 48 + 16 + oc
                op("dve", lambda e, oc=oc, bank=bank, go=go: e.scalar_tensor_tensor(
                    out=xT[:, oc * G:(oc + 1) * G], in0=bank[:, 0:G], scalar=modv[:, go:go + 1], in1=xT[:, oc * G:(oc + 1) * G],
                    op0=ALU.mult, op1=ALU.add), [bn, "modv%d" % l, XN(oc)], [XN(oc)])
        if STOP.get('s', 99) == 7:
            return
        norm_to_hT(l, 2, xT, XN)
        if STOP.get('s', 99) == 8:
            return
        if g == 0:
            op("pool", lambda e: e.memset(halo[:], 0.0), [], ["halo"])
        cwo = l * LS + O_CW
        cbo = l * LS + O_CB
        def ffn_up(q):
            npair = 2 if q < 5 else 1
            ai = q % 2
            at, atn = actT[ai], "actT%d" % ai
            for pr in range(npair):
                j0 = q * 4 + pr * 2
                ua_u, ua_n = load_unit(wsrc(wupb[l], 0, D, j0 * 128, 256), KC, 256, ["wupb%d" % l])
                ug_u, ug_n = load_unit(wsrc(wupb[l], 0, D, DFF + j0 * 128, 256), KC, 256, ["wupb%d" % l])
                for cc in range(2):
                    j = j0 + cc
                    jj = pr * 2 + cc
                    ba, ban = gbank()
                    mm_group(ba[:, 0:G], [(ua_u[:, kc * 256 + cc * 128:kc * 256 + (cc + 1) * 128], hT[:, kc * G:(kc + 1) * G]) for kc in range(KC)],
                             HT_ALL + [ua_n], [ban])
                    ua, uan = tmp()
                    op("pool", lambda e, ua=ua, j=j: e.tensor_copy(out=ua[:, 0:2], in_=halo[:, 2 * j:2 * j + 2]), ["halo"], [uan])
                    op("act", lambda e, ua=ua, ba=ba: e.activation(out=ua[:, 2:2 + G], in_=ba[:, 0:G], func=AF.Copy), [ban, uan], [uan])
                    op("pool", lambda e, ua=ua, j=j: e.tensor_copy(out=halo[:, 2 * j:2 * j + 2], in_=ua[:, G:G + 2]), [uan], ["halo"])
                    cv, cvn = tmp()
                    op("dve", lambda e, ua=ua, cv=cv, j=j: e.tensor_scalar(
                        out=cv[:, 0:G], in0=ua[:, 2:2 + G], scalar1=smalls[:, cwo + 2 * NJ + j:cwo + 2 * NJ + j + 1],
                        scalar2=smalls[:, cbo + j:cbo + j + 1], op0=ALU.mult, op1=ALU.add), [uan, "smalls"], [cvn])
                    op("dve", lambda e, ua=ua, cv=cv, j=j: e.scalar_tensor_tensor(
                        out=cv[:, 0:G], in0=ua[:, 1:1 + G], scalar=smalls[:, cwo + NJ + j:cwo + NJ + j + 1], in1=cv[:, 0:G],
                        op0=ALU.mult, op1=ALU.add), [uan, "smalls", cvn], [cvn])
                    op("dve", lambda e, ua=ua, cv=cv, j=j: e.scalar_tensor_tensor(
                        out=cv[:, 0:G], in0=ua[:, 0:G], scalar=smalls[:, cwo + j:cwo + j + 1], in1=cv[:, 0:G],
                        op0=ALU.mult, op1=ALU.add), [uan, "smalls", cvn], [cvn])
                    op("act", lambda e, cv=cv: e.activation(out=cv[:, 0:G], in_=cv[:, 0:G], func=AF.Gelu), [cvn], [cvn])
                    bg, bgn = gbank()
                    mm_group(bg[:, 0:G], [(ug_u[:, kc * 256 + cc * 128:kc * 256 + (cc + 1) * 128], hT[:, kc * G:(kc + 1) * G]) for kc in range(KC)],
                             HT_ALL + [ug_n], [bgn])
                    op("dve", lambda e, cv=cv, bg=bg, jj=jj, at=at: e.tensor_tensor(out=at[:, jj * G:(jj + 1) * G], in0=bg[:, 0:G], in1=cv[:, 0:G], op=ALU.mult),
                       [bgn, cvn], [atn])

        def ffn_down(q):
            npair = 2 if q < 5 else 1
            ai = q % 2
            at, atn = actT[ai], "actT%d" % ai
            nkc = npair * 2
            dus = []
            for pr in range(npair):
                j0 = q * 4 + pr * 2
                dus.append(load_unit(wsrc(wdnb[l], j0 * 128, 256, 0, D), 2, D, ["wdnb%d" % l]))
            for oc in range(KC):
                bank, bn = gbank()
                pairs = []
                for jj in range(nkc):
                    du = dus[jj // 2][0]
                    pairs.append((du[:, (jj % 2) * D + oc * 128:(jj % 2) * D + (oc + 1) * 128], at[:, jj * G:(jj + 1) * G]))
                mm_group(bank[:, 0:G], pairs, [atn] + [d_[1] for d_ in dus], [bn])
                go = l * 48 + 40 + oc
                op("dve", lambda e, oc=oc, bank=bank, go=go: e.scalar_tensor_tensor(
                    out=xT[:, oc * G:(oc + 1) * G], in0=bank[:, 0:G], scalar=modv[:, go:go + 1], in1=xT[:, oc * G:(oc + 1) * G],
                    op0=ALU.mult, op1=ALU.add), [bn, "modv%d" % l, XN(oc)], [XN(oc)])

        ffn_up(0)
        for q in range(6):
            if q + 1 < 6:
                ffn_up(q + 1)
            ffn_down(q)
        if STOP.get('s', 99) == 9:
            return
        if not last_layer:
            for kc in range(KC):
                S_.dma("pool", "x", out=xT_d[kc * P:(kc + 1) * P, t0:t0 + G], in_=xT[:, kc * G:(kc + 1) * G],
                       reads=[XN(kc)], writes=["xTd%d_%d" % (g, kc)])
        else:
            bank, bn = gbank()
            for kc in range(KC):
                sq, sn = (sqbuf() if STOP.get("F", 1) else tmp())
                op("act", lambda e, kc=kc, sq=sq: e.activation(out=sq[:, 0:G], in_=xT[:, kc * G:(kc + 1) * G], func=AF.Square), [XN(kc)], [sn])
                op("pe", lambda e, kc=kc, sq=sq: e.matmul(bank[:, 0:G], lhsT=(ones_b[:] if STOP.get("F", 1) else ones_f), rhs=sq[:, 0:G], start=(kc == 0), stop=(kc == KC - 1)),
                   [sn, "ones_b", "consts"], [bn])
            rs, rn = tmp()
            op("act", lambda e: e.activation(out=rs[:, 0:G], in_=bank[:, 0:G], func=AF.Sqrt, bias=eps_ap, scale=1.0), [bn, "consts"], [rn])
            op("dve", lambda e: e.reciprocal(out=rs[:, 0:G], in_=rs[:, 0:G]), [rn], [rn])
            for kc in range(KC):
                op("dve", lambda e, kc=kc: e.scalar_tensor_tensor(out=xT[:, kc * G:(kc + 1) * G], in0=xT[:, kc * G:(kc + 1) * G],
                                                                 scalar=smalls[:, O_FG + kc:O_FG + kc + 1], in1=rs[:, 0:G],
                                                                 op0=ALU.mult, op1=ALU.mult), [XN(kc), rn, "smalls"], [XN(kc)])
            for b in range(4):
                for hf in range(2):
                    bank, bn = gbank()
                    rd = [XN(hf * 4 + cc) for cc in range(4)] + ["consts"]
                    S_.begin("pe", rd, [bn])
                    inst = None
                    for cc in range(4):
                        kc = hf * 4 + cc
                        inst = nc.tensor.transpose(out=bank[:, cc * 128:(cc + 1) * 128], in_=xT[:, kc * G + b * 128:kc * G + (b + 1) * 128], identity=ident_f)
                    S_.end("pe", inst, rd, [bn])
                    tb, tn = tmp()
                    evac(tb[:, 0:512], bank[:, 0:512], [bn], [tn])
                    S_.dma("pool", "x", out=out_d[t0 + b * 128:t0 + (b + 1) * 128, hf * 512:(hf + 1) * 512], in_=tb[:, 0:512],
                           reads=[tn], writes=["outd"])

    per_g = (48 + TG - 1) // TG
    prepare_x(0, 0, 0)
    norm_stats(xTb[0], (lambda kc: "xT0_%d" % kc), rsn, "rsn")
    for l in range(L):
        for g in range(TG):
            if l + 1 < L:
                parts = [g] if g + 1 < TG else list(range(g, 6))
                emit_convert(l + 1, [q_ for q_ in parts if q_ < 6])
            cis = list(range(g * per_g, min(48, (g + 1) * per_g))) if l + 1 < L else []
            pre = 0 < len(cis) <= 6
            if pre:
                c1, c2 = cis[:3], cis[3:]
                emit_mod_load(l + 1, c1)

                def hook(l=l, c1=c1, c2=c2):
                    emit_mod_mm(l + 1, c1)
                    emit_mod_load(l + 1, c2)
                layer_group(l, g, hook)
                emit_mod_mm(l + 1, c2)
            else:
                layer_group(l, g)
                if l + 1 < L:
                    emit_mod(l + 1, cis)
        if l + 1 < L:
            emit_der(l + 1)
    if dbg_d is not None:
        S_.dma("sp", "m", out=dbg_d[:, 0:48], in_=modv[:, 0:48], reads=["modv0"], writes=["dbgd"])
        S_.dma("sp", "m", out=dbg_d[:, 48:64], in_=der[:, 0:16], reads=["der0"], writes=["dbgd2"])
    S_.wait_all("pool")
    S_.wait_all("sp")
    es.close()
    return nc


def make_consts():
    c = np.zeros((P, NCON), np.float32)
    c[:, C_ID:C_ID + 128] = np.eye(128, dtype=np.float32)
    c[:, C_ONES:C_ONES + 128] = 1.0 / 1024.0
    si = np.arange(128)[:, None]
    qi = np.arange(128)[None, :]
    c[:, C_TRI:C_TRI + 128] = (si <= qi).astype(np.float32)
    for h in range(4):
        slope = 2.0 ** (-8.0 * (h + 1) / 4)
        for di in range(ND):
            dist = di - 3
            c[:, C_DB + h * ND + di] = slope * (np.arange(128) - 127) - slope * 128.0 * dist
    c[:, C_EPS] = EPS
    sw = np.zeros((P, 2, 2, 4, 128), np.float32)
    for which in range(2):
        for half in range(2):
            for cc in range(4):
                h = 2 * cc + half
                slope = 2.0 ** (-(h + 1))
                if which == 1:
                    delta = qi - si
                    valid = delta >= 0
                else:
                    delta = qi + 128 - si
                    valid = delta < 128
                sw[:, which, half, cc, :] = np.where(valid, -slope * delta, -30000.0)
    return c, sw.reshape(P, 2048)


def pm(v):
    v = np.asarray(v, np.float32)
    return np.ascontiguousarray(v.reshape(-1, P).T)


def make_smalls(b, inp, L):
    NS = L * LS + 16 + 8
    s = np.zeros((P, NS), np.float32)
    for l in range(L):
        o = l * LS
        s[:, o + O_G1:o + O_G1 + 8] = pm(inp["norm1_g"][l])
        s[:, o + O_G2:o + O_G2 + 8] = pm(inp["norm2_g"][l])
        for k in range(3):
            s[:, o + O_CW + k * NJ:o + O_CW + (k + 1) * NJ] = pm(inp["conv_w"][l, k])
        s[:, o + O_CB:o + O_CB + NJ] = pm(inp["conv_b"][l])
        s[:, o + O_BADA:o + O_BADA + 48] = pm(inp["b_ada"][l])
        s[:, o + O_SINK:o + O_SINK + 8] = np.broadcast_to(np.asarray(inp["swa_sinks"][l], np.float32)[None, :], (P, 8))
        s[:, o + O_LAM:o + O_LAM + 256] = np.broadcast_to(np.asarray(inp["diff_lambda"][l], np.float32).reshape(1, 256), (P, 256))
        s[:, o + O_SUBG] = np.asarray(inp["diff_subln_g"][l], np.float32)
    cc = pm(inp["c"][b])
    s[:, L * LS:L * LS + 16] = np.repeat(cc, 2, axis=1)
    s[:, L * LS + 16:L * LS + 24] = pm(inp["final_g"])
    return s


_CACHE = {}


def run(inp, S, L, n_cores):
    key = (S, L)
    if key not in _CACHE:
        _CACHE[key] = build(S, L)
    nc = _CACHE[key]
    consts, swab = make_consts()
    f = lambda a: np.ascontiguousarray(np.asarray(a, np.float32))
    shared = {
        "consts": consts, "swab": swab,
        "w_ada": f(inp["w_ada"][:L]), "w_in": f(inp["w_in"][:L]), "w_bs": f(inp["w_branch_swa"][:L]),
        "w_bd": f(inp["w_branch_diff"][:L]), "w_out": f(inp["w_out"][:L]), "w_up": f(inp["w_up"][:L]),
        "w_dn": f(inp["w_down"][:L]),
    }
    real_slots = [0, 1, 4, 5][:n_cores] if (n_cores == 4 and not STOP.get("nodummy")) else list(range(n_cores))
    n_launch = 8 if real_slots != list(range(n_cores)) else n_cores
    zero_shared = None
    in_maps = []
    for slot in range(n_launch):
        if slot in real_slots:
            b = real_slots.index(slot)
            m = dict(shared)
            m["x"] = f(inp["x"][b, :S])
            m["smalls"] = make_smalls(b, inp, L)
        else:
            if zero_shared is None:
                zero_shared = {k: np.zeros_like(v) for k, v in shared.items()}
                zero_shared["consts"] = consts
                zero_shared["swab"] = swab
            m = dict(zero_shared)
            m["x"] = np.zeros((S, D), np.float32)
            m["smalls"] = np.zeros((P, L * LS + 24), np.float32)
        in_maps.append(m)
    res = run_bass_kernel_spmd(nc, in_maps, core_ids=list(range(n_launch)))
    if STOP.get("dbg"):
        STOP["dbg_out"] = [np.asarray(r["dbg"]) for r in res.results]
    return np.stack([np.asarray(res.results[slot]["out"], np.float32) for slot in real_slots], axis=0)


def kernel(**inputs):
    inp = {k: np.asarray(v) for k, v in inputs.items()}
    B, S, _ = inp["x"].shape
    L = inp["w_in"].shape[0]
    return run(inp, S, L, B)
```
